# Optimizing a Trainium2 kernel written in Bass

```python
import jax, jax.numpy as jnp
from jax import lax
import numpy as np

D_MODEL = 1024
BATCH = 8
SEQ = 2048
DEPTH = 1

HEAD_DIM = 64
N_FOX_HEADS = 8
N_SWA_HEADS = 8
N_SWA_KV_HEADS = 2
SWA_GROUP = N_SWA_HEADS // N_SWA_KV_HEADS
D_FOX = N_FOX_HEADS * HEAD_DIM
D_SWA = N_SWA_HEADS * HEAD_DIM
D_SWA_KV = N_SWA_KV_HEADS * HEAD_DIM
D_MIX = D_FOX + D_SWA
D_IN = 3 * D_FOX + N_FOX_HEADS + D_SWA + 2 * D_SWA_KV
D_FF = 4 * D_MODEL
D_PLE = 256
WINDOW = 128
Q_BLOCK = 128
N_BUCKETS = 32
MAX_DISTANCE = 128
RMS_EPS = 1e-6

kernel_name = "hymba_fox_swa_sandwich_ple_layer"


def rms_norm(x, g):
    xf = x.astype(jnp.float32)
    y = xf * lax.rsqrt(jnp.mean(xf * xf, axis=-1, keepdims=True) + RMS_EPS)
    return (y * g.astype(jnp.float32)).astype(x.dtype)


def t5_bucket(n):
    max_exact = N_BUCKETS // 2
    large = max_exact + (np.log(np.maximum(n, 1) / max_exact)
                         / np.log(MAX_DISTANCE / max_exact)
                         * (N_BUCKETS - max_exact)).astype(np.int32)
    large = np.minimum(large, N_BUCKETS - 1)
    return np.where(n < max_exact, n, large).astype(np.int32)


def forgetting_attention(q, k, v, log_f):
    B, S, H, D = q.shape
    c = jnp.cumsum(log_f, axis=1).transpose(0, 2, 1)
    scale = D ** -0.5
    outs = []
    for blk in range(S // Q_BLOCK):
        lo, hi = blk * Q_BLOCK, (blk + 1) * Q_BLOCK
        s = jnp.einsum('bqhd,bkhd->bhqk', q[:, lo:hi], k[:, :hi]).astype(jnp.float32) * scale
        s = s + c[:, :, lo:hi, None] - c[:, :, None, :hi]
        causal = np.arange(lo, hi)[:, None] >= np.arange(hi)[None, :]
        s = jnp.where(causal, s, -jnp.inf)
        probs = jax.nn.softmax(s, axis=-1).astype(v.dtype)
        outs.append(jnp.einsum('bhqk,bkhd->bqhd', probs, v[:, :hi]))
    return jnp.concatenate(outs, axis=1).reshape(B, S, H * D)


def sliding_window_attention(q, k, v, sinks, rel_bias):
    B, S, Hq, D = q.shape
    Hkv, G, Q = N_SWA_KV_HEADS, SWA_GROUP, Q_BLOCK
    nb = S // Q
    qb = q.reshape(B, nb, Q, Hkv, G, D)

    def band(t):
        pad = jnp.pad(t, ((0, 0), (Q, 0), (0, 0), (0, 0)))
        prev = pad[:, :S].reshape(B, nb, Q, Hkv, D)
        cur = t.reshape(B, nb, Q, Hkv, D)
        return jnp.concatenate([prev, cur], axis=2)

    kb, vb = band(k), band(v)
    s = jnp.einsum('bnqkgd,bnskd->bnkgqs', qb, kb).astype(jnp.float32) * (D ** -0.5)
    i = np.arange(Q)[:, None]
    j = np.arange(2 * Q)[None, :]
    dist = i + Q - j
    in_window = (dist >= 0) & (dist < WINDOW)
    bucket = t5_bucket(np.clip(dist, 0, None))
    bias = jnp.transpose(rel_bias[bucket], (2, 0, 1)).reshape(Hkv, G, Q, 2 * Q)
    s = s + bias.astype(jnp.float32)
    key_pos = np.arange(nb)[:, None] * Q - Q + np.arange(2 * Q)[None, :]
    valid = in_window[None, :, :] & (key_pos >= 0)[:, None, :]
    s = jnp.where(valid[None, :, None, None], s, -jnp.inf)
    sink = sinks.astype(jnp.float32).reshape(Hkv, G)[None, None, :, :, None, None]
    m = jnp.maximum(jnp.max(s, axis=-1, keepdims=True), sink)
    e = jnp.exp(s - m)
    denom = jnp.sum(e, axis=-1, keepdims=True) + jnp.exp(sink - m)
    probs = (e / denom).astype(v.dtype)
    o = jnp.einsum('bnkgqs,bnskd->bnqkgd', probs, vb)
    return o.reshape(B, S, Hq * D)


def setup_inputs(seed: int = 0) -> dict:
    key = jax.random.key(seed)
    ks = jax.random.split(key, 20)
    f32 = jnp.float32
    nrm = lambda k, shape, s: jax.random.normal(k, shape, f32) * s
    gain = lambda k, shape: 1.0 + 0.05 * jax.random.normal(k, shape, f32)
    return {
        "x": jax.random.normal(ks[0], (BATCH, SEQ, D_MODEL), f32),
        "p": jax.random.normal(ks[1], (DEPTH, BATCH, SEQ, D_PLE), f32),
        "w_in": nrm(ks[2], (DEPTH, D_MODEL, D_IN), D_MODEL ** -0.5),
        "b_forget": 1.0 + 0.5 * jax.random.normal(ks[3], (DEPTH, N_FOX_HEADS), f32),
        "w_out": nrm(ks[4], (DEPTH, D_MIX, D_MODEL), D_MIX ** -0.5),
        "rel_bias": nrm(ks[5], (N_BUCKETS, N_SWA_HEADS), 0.5),
        "swa_sinks": nrm(ks[6], (DEPTH, N_SWA_HEADS), 0.5),
        "g_attn_pre": gain(ks[7], (DEPTH, D_MODEL)),
        "g_attn_post": gain(ks[8], (DEPTH, D_MODEL)),
        "w_ff1": nrm(ks[9], (DEPTH, D_MODEL, D_FF), D_MODEL ** -0.5),
        "w_ff2": nrm(ks[10], (DEPTH, D_FF, D_MODEL), D_FF ** -0.5),
        "g_ff_pre": gain(ks[11], (DEPTH, D_MODEL)),
        "g_ff_post": gain(ks[12], (DEPTH, D_MODEL)),
        "w_ple": nrm(ks[13], (DEPTH, D_PLE, D_MODEL), D_PLE ** -0.5),
        "w_ple_gate": nrm(ks[14], (DEPTH, D_MODEL, D_MODEL), D_MODEL ** -0.5),
        "g_ple_post": gain(ks[15], (DEPTH, D_MODEL)),
    }


def reference(x, p, w_in, b_forget, w_out, rel_bias, swa_sinks, g_attn_pre, g_attn_post,
              w_ff1, w_ff2, g_ff_pre, g_ff_post, w_ple, w_ple_gate, g_ple_post):
    B, S, _ = x.shape
    h = x
    splits = np.cumsum([D_FOX, D_FOX, D_FOX, N_FOX_HEADS, D_SWA, D_SWA_KV])
    for i in range(DEPTH):
        a = rms_norm(h, g_attn_pre[i])
        z = a @ w_in[i]
        fq, fk, fv, ff, sq, sk, sv = jnp.split(z, splits, axis=-1)
        log_f = jax.nn.log_sigmoid(ff.astype(jnp.float32) + b_forget[i].astype(jnp.float32))
        fox = forgetting_attention(
            fq.reshape(B, S, N_FOX_HEADS, HEAD_DIM),
            fk.reshape(B, S, N_FOX_HEADS, HEAD_DIM),
            fv.reshape(B, S, N_FOX_HEADS, HEAD_DIM), log_f)
        swa = sliding_window_attention(
            sq.reshape(B, S, N_SWA_HEADS, HEAD_DIM),
            sk.reshape(B, S, N_SWA_KV_HEADS, HEAD_DIM),
            sv.reshape(B, S, N_SWA_KV_HEADS, HEAD_DIM), swa_sinks[i], rel_bias)
        mix = jnp.concatenate([fox, swa], axis=-1) @ w_out[i]
        h = h + rms_norm(mix, g_attn_post[i])
        m = rms_norm(h, g_ff_pre[i])
        y = jnp.square(jax.nn.relu(m @ w_ff1[i])) @ w_ff2[i]
        h = h + rms_norm(y, g_ff_post[i])
        gate = jax.nn.sigmoid(h @ w_ple_gate[i])
        e = (p[i].astype(h.dtype) @ w_ple[i]) * gate
        h = h + rms_norm(e, g_ple_post[i])
    return h
```

```python
import numpy as np
import concourse.bass as bass
import concourse.mybir as mybir
from concourse.bass_utils import run_bass_kernel_spmd

F32 = mybir.dt.float32
BF16 = mybir.dt.bfloat16
AF = mybir.ActivationFunctionType
ALU = mybir.AluOpType

S = 2048
D = 1024
NEG = -240000.0
EPS = 1e-6
ENGS = ["pe", "act", "dve", "pool", "sp"]


class Prog:
    def __init__(self):
        self.ops = []
        self.res = {}
        self.eng_ops = {e: [] for e in ENGS}
        self.dma_groups = {}
        self.barrier_deps = set()
        self.dma_since_barrier = []
        self.cur_buf = 0
        self.reg_members = {}
        self.reg_guard = {}

    UNITKEYS = ("Qe", "Qo", "KD", "VS", "qE", "qO", "kE", "kO", "VF")

    def km(self, k):
        if k[0] in self.UNITKEYS:
            return (k[0], self.cur_buf) + tuple(k[1:])
        return k

    @staticmethod
    def regions_of(k):
        n = k[0]
        if n == "a":
            return ("X1",) if k[1] < 4 else ("X2",)
        if n in ("m", "hb"):
            return ("X1",)
        if n == "mix":
            return ("X2",)
        if n == "tmp":
            return ("ZO_O",)
        if n in ("t1", "t2", "a3", "OT", "mixb", "hb2"):
            return ("ZO_O",)
        if n == "u":
            return ("ZO_O",) if k[1] < 16 else ("B0", "BMR")
        if n in Prog.UNITKEYS:
            return ("B%d" % k[1],)
        if n == "c" and k[1] == "bm":
            return ("BMR",)
        if n == "h" and k[1] < 4:
            return ("B1",)
        if n in ("den", "rt"):
            return ("DENR",)
        if n in ("rc", "gt", "rs2"):
            return ("RCR",)
        return ()

    def retire(self, R):
        mem = self.reg_members.get(R, [])
        if not mem or not self.enabled:
            return
        idx = len(self.ops)
        deps = set(mem)
        if self.reg_guard.get(R) is not None:
            deps.add(self.reg_guard[R])
        self.ops.append(dict(eng="sp", fn=lambda e: e.nop(), deps=deps, dma=None, sig=False, val=0,
                             cost=60.0, lat=0.0, alt=None, ndma=1))
        self.eng_ops["sp"].append(idx)
        self.reg_guard[R] = idx
        self.reg_members[R] = []

    enabled = True

    def add(self, eng, fn, reads=(), writes=(), dma=None, nobarrier=False, cost=300.0, lat=0.0, alt=None):
        if not self.enabled:
            self.ops_dummy = dict(ndma=1)
            return None
        idx = len(self.ops)
        reads = [self.km(k) for k in reads]
        writes = [self.km(k) for k in writes]
        ex = [k for k in reads if k[0] == "ps"]
        if ex:
            reads = [k for k in reads if k[0] != "ps"]
            writes = list(writes) + [k for k in ex if k not in writes]
        deps = set() if nobarrier else set(self.barrier_deps)
        for r in reads:
            st = self.res.get(r)
            if st is not None and st[0] is not None:
                deps.add(st[0])
        for w in writes:
            st = self.res.get(w)
            if st is not None:
                if st[0] is not None:
                    deps.add(st[0])
                deps.update(st[1])
        for r in reads:
            st = self.res.setdefault(r, [None, []])
            st[1].append(idx)
        for w in writes:
            self.res[w] = [idx, []]
        if dma is not None:
            g = self.dma_groups.setdefault(dma, [])
            if g:
                deps.add(g[-1])
            g.append(idx)
            if not nobarrier:
                self.dma_since_barrier.append(idx)
        for k in list(reads) + list(writes):
            for R in self.regions_of(k):
                g = self.reg_guard.get(R)
                if g is not None:
                    deps.add(g)
                self.reg_members.setdefault(R, []).append(idx)
        deps.discard(idx)
        self.ops.append(dict(eng=eng, fn=fn, deps=deps, dma=dma, sig=False, val=0, cost=cost, lat=lat, alt=alt))
        self.eng_ops[eng].append(idx)
        return idx

    def schedule(self):
        ops = self.ops
        n = len(ops)
        ndeps = [len(op["deps"]) for op in ops]
        users = [[] for _ in range(n)]
        for i, op in enumerate(ops):
            for d in op["deps"]:
                users[d].append(i)
        rt = [0.0] * n
        fin = [0.0] * n
        free = {e: 0.0 for e in ENGS}
        order = {e: [] for e in ENGS}
        avail = [i for i in range(n) if ndeps[i] == 0]
        import os
        SYNC = float(os.environ.get("K_SYNC", "120"))
        WIN = float(os.environ.get("K_WIN", "0"))
        PRIO = os.environ.get("K_PRIO", "bl")
        bl = [0.0] * n
        for i in range(n - 1, -1, -1):
            m = 0.0
            for u in users[i]:
                if bl[u] > m:
                    m = bl[u]
            bl[i] = ops[i]["cost"] + ops[i]["lat"] + m
        dma_free = 0.0
        import os
        ACT_PEN = float(os.environ.get("ACT_PEN", "300"))
        done = 0
        while avail:
            cands = []
            mins = None
            for i in avail:
                op = ops[i]
                best = None
                opts = [(op["eng"], op["cost"], None)]
                if op["alt"] is not None:
                    opts.append((op["alt"][0], op["alt"][2], op["alt"]))
                bestv = None
                for (e, c, a) in opts:
                    st = max(rt[i], free[e])
                    v = st + c + (op.get("pen", 0.0) if (len(opts) > 1 and e == "act") else 0.0)
                    if best is None or v < bestv:
                        best = (st, e, c, a)
                        bestv = v
                cands.append((best[0], i, best))
                if mins is None or best[0] < mins:
                    mins = best[0]
            pick = None
            for (st, i, b) in cands:
                if st <= mins + WIN:
                    if pick is None:
                        pick = (st, i, b)
                    elif PRIO == "bl":
                        if bl[i] > bl[pick[1]]:
                            pick = (st, i, b)
                    elif i < pick[1]:
                        pick = (st, i, b)
            st, i, (st_, e, c, a) = pick
            op = ops[i]
            if a is not None:
                op["eng"] = a[0]
                op["fn"] = a[1]
                op["cost"] = a[2]
            order[e].append(i)
            free[e] = st + c
            if op["dma"] is not None:
                d0 = max(st + c, dma_free)
                dma_free = d0 + op.get("nbytes", 0) / 330.0
                fin[i] = dma_free + 2000.0
            else:
                fin[i] = st + c + op["lat"]
            avail.remove(i)
            for u in users[i]:
                ndeps[u] -= 1
                same_pe = (e == "pe" and ops[u]["eng"] == "pe" and ops[u]["alt"] is None)
                t = fin[i] + (0.0 if same_pe else SYNC)
                if t > rt[u]:
                    rt[u] = t
                if ndeps[u] == 0:
                    avail.append(u)
            done += 1
        assert done == n, (done, n)
        self.eng_ops = order
        self.makespan = max(fin) if fin else 0.0
        busy = {e: sum(ops[i]["cost"] for i in order[e]) for e in ENGS}
        print("[sched] est makespan %.1f us; busy us: %s" % (self.makespan / 1e3, {e: round(busy[e] / 1e3, 1) for e in ENGS}))

    def barrier(self):
        deps = set()
        for e in ["pe", "act", "dve", "sp"]:
            if self.eng_ops[e]:
                deps.add(self.eng_ops[e][-1])
        deps.update(self.dma_since_barrier)
        self.dma_since_barrier = []
        self.barrier_deps = deps

    def emit(self, nc, stack):
        ops = self.ops
        waited = set()
        for op in ops:
            for d in op["deps"]:
                if ops[d]["dma"] is None and ops[d]["eng"] == "pe" and op["eng"] == "pe":
                    continue
                waited.add(d)
        cnt = {e: 0 for e in ENGS}
        dcnt = {}
        for e in ENGS:
            for i in self.eng_ops[e]:
                op = ops[i]
                assert op["eng"] == e
                if op["dma"] is None:
                    if i in waited:
                        cnt[e] += 1
                        op["sig"] = True
                    op["val"] = cnt[e]
        engsem = {e: stack.enter_context(nc.semaphore("s_" + e)) for e in ENGS}
        dmasem = {g: stack.enter_context(nc.semaphore("d_%d" % i)) for i, g in enumerate(self.dma_groups)}
        dmaval = {g: 0 for g in self.dma_groups}
        block = stack.enter_context(nc.Block())

        def run(ename, eng):
            seen = {}
            for i in self.eng_ops[ename]:
                op = ops[i]
                for d in sorted(op["deps"]):
                    dop = ops[d]
                    if dop["dma"] is not None:
                        key = ("dma", dop["dma"])
                        sem = dmasem[dop["dma"]]
                    else:
                        if dop["eng"] == "pe" and ename == "pe":
                            continue
                        key = ("eng", dop["eng"])
                        sem = engsem[dop["eng"]]
                    v = dop["val"]
                    assert v > 0, (d, i)
                    if seen.get(key, 0) >= v:
                        continue
                    eng.wait_ge(sem, v)
                    seen[key] = v
                r = op["fn"](eng)
                if op["dma"] is not None:
                    insts = r if isinstance(r, (list, tuple)) else [r]
                    for ins in insts:
                        ins.then_inc(dmasem[op["dma"]], 16)
                elif op["sig"]:
                    r.then_inc(engsem[ename], 1)

        for i, op in enumerate(ops):
            if op["dma"] is not None:
                dmaval[op["dma"]] += 16 * op["ndma"]
                op["val"] = dmaval[op["dma"]]

        @block.tensor
        def _(e):
            run("pe", e)

        @block.scalar
        def _(e):
            run("act", e)

        @block.vector
        def _(e):
            run("dve", e)

        @block.gpsimd
        def _(e):
            run("pool", e)

        @block.sync
        def _(e):
            run("sp", e)


def build_program(stop=99):
    nc = bass.Bass("TRN2", target_bir_lowering=False)
    pr = Prog()

    def dram(name, shape, dt=F32, kind="ExternalInput"):
        return nc.dram_tensor(name, shape, dt, kind=kind).ap()

    xT = dram("xT", [D, S])
    pT = dram("pT", [256, S])
    w_in = dram("w_in", [D, 2312])
    w_out = dram("w_out", [D, D])
    w_ff1 = dram("w_ff1", [D, 4096])
    w_ff2 = dram("w_ff2", [4096, D])
    w_ple = dram("w_ple", [256, D])
    w_gate = dram("w_gate", [D, D])
    gT = dram("gT", [128, 40])
    bfg = dram("bfg", [8, 1])
    sinks = dram("sinks", [128, 8])
    biasT = dram("biasT", [128, 2048])
    maskF = dram("maskF", [128, 128])
    identD = dram("ident", [128, 128])
    yT = dram("yT", [D, S], kind="ExternalOutput")
    augD = dram("augscr", [8, 3, S], BF16, kind="Internal")

    from contextlib import ExitStack

    stack = ExitStack()
    with stack:
        def sb(name, shape, dt):
            return stack.enter_context(nc.sbuf_tensor(name, shape, dt))

        hT = sb("hT", [128, 8, S], F32)
        X1 = sb("X1", [128, 8192], BF16)
        X2 = sb("X2", [128, 8192], BF16)
        ZO = sb("ZO", [128, 32768], BF16)
        NWS = 4
        WS = [sb("WS%d" % i, [128, 4096], BF16) for i in range(NWS)]
        G = sb("G", [128, 40], F32)
        IDB = sb("IDB", [128, 128], BF16)
        MFB = sb("MFB", [128, 128], BF16)
        ONESB = sb("ONESB", [128, 128], BF16)
        SWAP = sb("SWAP", [128, 128], BF16)
        SKE = sb("SKE", [128, 4], F32)
        SK = sb("SK", [128, 8], F32)
        BFt = sb("BFt", [8, 1], F32)
        NBt = sb("NBt", [8, 1], F32)
        NRS = 1
        NSQ = 2
        NPT = 2
        RS = [sb("RS%d" % i, [128, 512], F32) for i in range(NRS)]
        SQ = [sb("SQ%d" % i, [128, 512], BF16) for i in range(NSQ)]
        PTP = [sb("PT%d" % i, [128, 2, 512], BF16) for i in range(NPT)]
        RE = sb("RE", [128, 512], BF16)
        RO = sb("RO", [128, 512], BF16)
        RT = [RE, RO]
        DEN = [RE, RO]
        DT = sb("DT", [128, 512], F32)
        BCSt = sb("BCS", [128, 1024], BF16)
        RC = [DT, BCSt.bitcast(F32)]
        GT = RC
        PSa = stack.enter_context(nc.psum_tensor("psa", [128, 1024], F32))
        PSb = stack.enter_context(nc.psum_tensor("psb", [128, 1024], F32))
        PS = [PSa[:, 0:512], PSa[:, 512:1024], PSb[:, 0:512], PSb[:, 512:1024]]
        PS += [stack.enter_context(nc.psum_tensor("ps%d" % i, [128, 512], F32))[:, :] for i in range(4, 8)]
        PSP = [PSa[:, :].rearrange("p (a x) -> p a x", a=2), PSb[:, :].rearrange("p (a x) -> p a x", a=2)]
        ACCA, ACCB = 4, 5

        X2f = X2.bitcast(F32)
        ZOf = ZO.bitcast(F32)
        MIX = X2f[:, 0:4096].rearrange("p (m t) -> p m t", m=8)

        def aT(k, c0, c1):
            t = X1 if k < 4 else X2
            kk = k % 4
            return t[:, kk * 2048 + c0: kk * 2048 + c1]

        OT = ZO[:, 0:16384].rearrange("p (c t) -> p c t", c=8)
        ZB = 16384
        BM = ZO[:, ZB + 13312: ZB + 15360]
        hTb = hT.bitcast(BF16)[:, :, :].rearrange("p k t -> p (k t)")
        UB = [ZO[:, ZB: ZB + 13312], hTb[:, 0:13312]]

        def swa_views(b):
            B_ = UB[b]
            return (B_[:, 0:4096].rearrange("p (c t) -> p c t", c=2),
                    B_[:, 4096:8192].rearrange("p (c t) -> p c t", c=2),
                    B_[:, 8192:10240],
                    B_[:, 10240:13312].rearrange("p (t c) -> p t c", t=16))

        def fox_views(b):
            B_ = UB[b]
            return (B_[:, 0:2048], B_[:, 2048:4096], B_[:, 4096:6144], B_[:, 6144:8192],
                    B_[:, 8192:11264].rearrange("p (t c) -> p t c", t=16))
        T1 = ZOf[0:8, 0:2048]
        T2 = ZOf[0:8, 2048:4096]
        A3 = ZO[0:8, 8192: 8192 + 6144].rearrange("p (r t) -> p r t", r=3)
        mT = X1[:, 0:8192].rearrange("p (k t) -> p k t", k=8)
        U = ZO[:, 0:32768].rearrange("p (f t) -> p f t", f=32)
        HB = X1[:, 0:4096].rearrange("p (k t) -> p k t", k=8)
        MIXB = ZOf[:, 0:4096].rearrange("p (m t) -> p m t", m=8)
        HB2 = ZO[:, 8192:12288].rearrange("p (k t) -> p k t", k=8)

        def phase(n):
            pr.enabled = n <= stop

        def nfree(ap):
            n = 1
            for d in list(ap.shape)[1:]:
                n *= int(d)
            return n

        def mm(out, lhsT, rhs, start, stop, reads, writes):
            c = max(nfree(rhs), 64) / 2.4 + 8.0
            return pr.add("pe", lambda e: e.matmul(out, lhsT, rhs, start=start, stop=stop,
                                                   skip_group_check=True), reads, writes, cost=c)

        def act_fn(out, in_, func, bias=None, scale=None):
            kw = {}
            if bias is not None:
                kw["bias"] = bias
            if scale is not None:
                kw["scale"] = scale
            return lambda e: e.activation(out, in_, func, **kw)

        def act_cost(out):
            return 210.0 + 0.833 * nfree(out)

        def act(out, in_, func, reads, writes, bias=None, scale=None):
            return pr.add("act", act_fn(out, in_, func, bias, scale), reads, writes, cost=act_cost(out))

        def dve_fn(fnname, *args, **kw):
            return lambda e: getattr(e, fnname)(*args, **kw)

        def dve_cost(out, fast=False):
            return 90.0 + nfree(out) * (0.52 if fast else 1.04)

        def dve(fnname, reads, writes, *args, **kw):
            fast = kw.pop("fast", False)
            return pr.add("dve", dve_fn(fnname, *args, **kw), reads, writes, cost=dve_cost(args[0], fast))

        cur_pen = [0.0]

        def anyop(reads, writes, out, act_args, dve_name, dve_args, fast=False):
            i = pr.add("act", act_fn(*act_args), reads, writes, cost=act_cost(out),
                       alt=("dve", dve_fn(dve_name, *dve_args), dve_cost(out, fast)))
            if i is not None:
                pr.ops[i]["pen"] = cur_pen[0]
            return i

        U32 = mybir.dt.uint32
        BITS = {1.0: 0x3F803F80, -1.0: 0xBF80BF80}

        def zero_fill(ap, reads, writes):
            n2 = nfree(ap) // 2
            i = pr.add("act", (lambda e: e.memzero(ap)), reads, writes, cost=160.0 + 0.833 * n2,
                       alt=("dve", (lambda e: e.memzero(ap)), 140.0 + 1.04 * n2))
            if i is not None:
                pr.ops[i]["pen"] = 0.0
            return i

        def const_fill(ap, val, reads, writes):
            apu = ap.bitcast(U32)
            return pr.add("dve", (lambda e: e.memset(apu, BITS[val])), reads, writes,
                          cost=140.0 + 1.04 * (nfree(ap) // 2))

        wcount = [0]

        def wload(dmas):
            s = wcount[0] % NWS
            wcount[0] += 1
            t = WS[s]

            def fn(e):
                r = []
                for ent in dmas:
                    if len(ent) == 2:
                        dst = ent[0](t)
                        src = ent[1]
                    else:
                        (c0, K, C, src) = ent
                        dst = t[:, c0: c0 + K * C].rearrange("p (k c) -> p k c", k=K)
                    r.append(e.dma_start(out=dst, in_=src))
                return r
            nb = 0
            for ent in dmas:
                nb += nfree(ent[1]) * 128 * 4 if len(ent) == 2 else ent[1] * ent[2] * 128 * 4
            rd = [("h", 0, 3)] if wcount[0] in (3, 4) else []
            i = pr.add("pool", fn, reads=rd, writes=[("ws", s)], dma="w%d" % s, nobarrier=True,
                       cost=1200.0 * len(dmas), lat=2500.0 + nb / 300.0)
            if i is not None:
                pr.ops[i]["nbytes"] = nb
            if i is not None:
                pr.ops[i]["ndma"] = len(dmas)
            return t, ("ws", s)

        def spdma(out, in_, reads, writes, group):
            i = pr.add("sp", lambda e: e.dma_start(out=out, in_=in_), reads, writes, dma=group,
                       cost=150.0, lat=2200.0 + nfree(out) * 128 * 4 / 200.0)
            if i is not None:
                pr.ops[i]["ndma"] = 1
                pr.ops[i]["nbytes"] = nfree(out) * int(out.shape[0]) * 4
            return i

        def wview(t, c0, K, C):
            return t[:, c0: c0 + K * C].rearrange("p (k c) -> p k c", k=K)

        def wsrc(w, r0, nrows, c0, c1):
            return w[r0: r0 + nrows, c0:c1].rearrange("(k p) c -> p k c", p=128)

        gen_i = [0]
        GEN = [6, 7]

        def genbank():
            b = GEN[gen_i[0] % len(GEN)]
            gen_i[0] += 1
            return b

        sq_i = [0]

        def sqtile():
            i = sq_i[0] % NSQ
            sq_i[0] += 1
            return i

        rs_i = [0]

        SQPOOL = [(SQ[i][:, :], ("sq", i)) for i in range(NSQ)]
        sqp_i = [0]

        def stats_add(src, src_reads, msb, first, last, sbsrc=False, n=512):
            sqt, sqk = SQPOOL[sqp_i[0] % len(SQPOOL)]
            sqt = sqt[:, 0:n]
            sqp_i[0] += 1
            if sbsrc:
                anyop(src_reads, [sqk], sqt, (sqt, src, AF.Square), "tensor_tensor", (sqt, src, src, ALU.mult))
            else:
                act(sqt, src, AF.Square, src_reads, [sqk])
            mm(PS[msb][:, 0:n], ONESB[:, :], sqt, first, last, [sqk, ("c", "ones")], [("ps", msb)])

        def rstd_from(msb, alt=False, eps=EPS, n=512):
            if alt:
                t, key = DT, ("rs2",)
            else:
                i = rs_i[0] % NRS
                rs_i[0] += 1
                t, key = RS[i], ("rs", i)
            dve("tensor_scalar", [("ps", msb)], [key], t[:, 0:n], PS[msb][:, 0:n], 1.0 / D, eps,
                ALU.mult, ALU.add)
            act(t[:, 0:n], t[:, 0:n], AF.Ln, [key], [key])
            act(t[:, 0:n], t[:, 0:n], AF.Exp, [key], [key], scale=-0.5)
            return t, key

        def prenorm(gidx, tg, dst_fn, dst_keys, alt=False):
            msb = genbank()
            for k in range(8):
                stats_add(hT[:, k, tg * 512:(tg + 1) * 512], [("h", k, tg), ("pno", k)], msb, k == 0, k == 7,
                          sbsrc=True)
            rt_, rk_ = rstd_from(msb, alt)
            for k in range(8):
                dve("scalar_tensor_tensor", [("h", k, tg), rk_, ("c", "g")], [dst_keys(k), ("pno", k)],
                    dst_fn(k), hT[:, k, tg * 512:(tg + 1) * 512], G[:, gidx * 8 + k: gidx * 8 + k + 1],
                    rt_[:, :], ALU.mult, ALU.mult)

        def postnorm_update(gidx, tg, msb, mixt=None, mkey="mix", after=None, eps=EPS, xreads=None,
                            t0=None, n=512, wkey=None):
            if mixt is None:
                mixt = MIX
            if t0 is None:
                t0 = tg * 512
            rt_, rk_ = rstd_from(msb, eps=eps, n=n)
            for mc in range(8):
                dve("scalar_tensor_tensor", [(mkey, mc), rk_, ("c", "g")] + (xreads(mc) if xreads else []),
                    [(mkey, mc)],
                    mixt[:, mc, 0:n], mixt[:, mc, 0:n], G[:, gidx * 8 + mc: gidx * 8 + mc + 1], rt_[:, 0:n],
                    ALU.mult, ALU.mult)
                wk = [("h", mc, tg)] if wkey is None else [wkey(mc)]
                dve("tensor_tensor", [(mkey, mc), ("h", mc, tg)], wk,
                    hT[:, mc, t0:t0 + n], hT[:, mc, t0:t0 + n], mixt[:, mc, 0:n], ALU.add)
                if after is not None:
                    after(mc)

        phase(0)
        xv = xT[:, :].rearrange("(k p) t -> p k t", p=128)
        def xload(t):
            o_ = hT[:, :, t * 512:(t + 1) * 512]
            i_ = xv[:, :, t * 512:(t + 1) * 512]
            i = pr.add("pool", (lambda e: e.dma_start(out=o_, in_=i_)), [], [("h", k, t) for k in range(8)],
                       dma="x", cost=1200.0, lat=2500.0)
            if i is not None:
                pr.ops[i]["ndma"] = 1
                pr.ops[i]["nbytes"] = 2 * 1024 * 1024
        spdma(G[:, :], gT[:, :], [], [("c", "g")], "c0")
        xload(0)
        spdma(BFt[:, :], bfg[:, :], [], [("c", "bf")], "c1")
        spdma(SK[:, :], sinks[:, :], [], [("c", "sk")], "c2")
        xload(1)
        xload(2)
        xload(3)
        spdma(ZOf[:, 0:128], identD[:, :], [], [("tmp", "id")], "c3")
        spdma(ZOf[:, 128:256], maskF[:, :], [], [("tmp", "mf")], "c3")
        i_bm = pr.add("pool", (lambda e: e.dma_start(out=BM, in_=biasT[:, :])), [], [("c", "bm")], dma="bm",
                      cost=1200.0, lat=2500.0)
        if i_bm is not None:
            pr.ops[i_bm]["ndma"] = 1
            pr.ops[i_bm]["nbytes"] = 1024 * 1024

        dve("memset", [], [("c", "ones")], ONESB[:, :], 1.0)

        phase(1)
        SQPOOL.extend([(RE[:, :], ("den", 0, "s", 0)), (RO[:, :], ("den", 1, "s", 0)),
                       (BCSt[:, 0:512], ("rc", "s", 0)), (BCSt[:, 512:1024], ("rc", "s", 1))])
        for tg in range(4):
            prenorm(0, tg, lambda k, tg=tg: aT(k, tg * 512, (tg + 1) * 512), lambda k, tg=tg: ("a", k, tg))
        del SQPOOL[NSQ:]
        pr.retire("DENR")
        pr.retire("RCR")
        LATE = [("a", 7, 3)]
        dve("tensor_copy", [("tmp", "id")] + LATE, [("c", "id")], IDB[:, :], ZOf[:, 0:128])
        dve("tensor_copy", [("tmp", "mf")] + LATE, [("c", "mf")], MFB[:, :], ZOf[:, 128:256])
        dve("tensor_copy", [("c", "id")], [("c", "swap")], SWAP[:, 0:64], IDB[:, 64:128])
        dve("tensor_copy", [("c", "id"), ("c", "swap")], [("c", "swap")], SWAP[:, 64:128], IDB[:, 0:64])
        dve("tensor_scalar", [("c", "bf")] + LATE, [("c", "nb")], NBt[:, :], BFt[:, :], -1.0, None, ALU.mult)
        act(SK[:, :], SK[:, :], AF.Exp, [("c", "sk")] + LATE, [("c", "sk")])
        for g in range(2):
            for c in range(2):
                he = 4 * g + 2 * c
                dve("tensor_copy", [("c", "sk"), ("c", "ske")], [("c", "ske")], SKE[64:128, g * 2 + c: g * 2 + c + 1],
                    SK[64:128, he:he + 1])
                dve("tensor_copy", [("c", "sk"), ("c", "ske")], [("c", "ske")], SKE[0:64, g * 2 + c: g * 2 + c + 1],
                    SK[0:64, he + 1:he + 2])
        pr.retire("ZO_O")

        a_all = [("a", k, t) for k in range(8) for t in range(4)]

        phase(2)
        wt, wkey = wload([(0, 8, 128, wsrc(w_in, 0, D, 1536, 1664))])
        wv = wview(wt, 0, 8, 128)
        for tg in range(4):
            b = genbank()
            for k in range(8):
                mm(PS[b][:, :], wv[:, k, :], aT(k, tg * 512, (tg + 1) * 512), k == 0, k == 7,
                   [wkey, ("a", k, tg)], [("ps", b)])
            act(T1[:, tg * 512:(tg + 1) * 512], PS[b][0:8, :], AF.Exp, [("ps", b), ("c", "nb")], [("t1",)],
                bias=NBt[:, 0:1], scale=-1.0)
        act(T1, T1, AF.Ln, [("t1",)], [("t1",)], bias=1.0, scale=1.0)
        dve("tensor_tensor_scan", [("t1",)], [("t2",)], T2, T1, T1, 0.0, ALU.add, ALU.max)
        dve("tensor_copy", [("t2",)], [("a3", 0)], A3[:, 0, :], T2)
        dve("tensor_tensor", [("t2",), ("a3", 0)], [("t1",)], T1, T2, A3[:, 0, :], ALU.subtract)
        dve("tensor_copy", [("t1",)], [("a3", 1)], A3[:, 1, :], T1)
        dve("tensor_tensor", [("t1",), ("a3", 1)], [("t2",)], T2, T1, A3[:, 1, :], ALU.subtract)
        dve("tensor_copy", [("t2",)], [("a3", 2)], A3[:, 2, :], T2)
        spdma(augD[:, :, :], A3, [("a3", 0), ("a3", 1), ("a3", 2)], [("augD",)], "aug")
        pr.retire("ZO_O")

        phase(3)
        cur_pen[0] = 300.0
        sbanks = [0, 1, 2]
        s_i = [0]
        sp_i = [0]
        pt_i = [0]
        ev_i = [0]

        def evac_scaled(out, in_, reads, writes, scale):
            anyop(reads, writes, out, (out, in_, AF.Copy, None, scale), "tensor_scalar",
                  (out, in_, scale, None, ALU.mult))

        for g in range(2):
            phase(3)
            pr.cur_buf = [0, 1][g]
            if g == 1:
                pr.retire("B1")
            Qe, Qo, KD, VS = swa_views(pr.cur_buf)
            zero_fill(Qe[64:128, :, :], [], [("Qe", "pad")])
            zero_fill(Qo[0:64, :, :], [], [("Qo", "pad")])
            const_fill(VS[:, :, 64:128], 1.0, [], [("VS", "ones")])
            qc0 = 1544 + 256 * g
            kc0 = 2056 + 64 * g
            vc0 = 2184 + 64 * g
            wt, wkey = wload([(0, 8, 256, wsrc(w_in, 0, D, qc0, qc0 + 256)),
                              (lambda t: t[:, 2048:3072].rearrange("p (k c) -> p k c", k=8)[:, :, 0:64],
                               wsrc(w_in, 0, D, kc0, kc0 + 64)),
                              (lambda t: t[:, 2048:3072].rearrange("p (k c) -> p k c", k=8)[:, :, 64:128],
                               wsrc(w_in, 0, D, kc0, kc0 + 64)),
                              (3072, 8, 64, wsrc(w_in, 0, D, vc0, vc0 + 64))])
            wq = wview(wt, 0, 8, 256)
            wkk = wview(wt, 2048, 8, 128)
            wvv = wview(wt, 3072, 8, 64)
            phase(3.02)
            for c in range(2):
                for tg in range(4):
                    b = genbank()
                    for k in range(8):
                        mm(PS[b][:, :], wq[:, k, c * 128:(c + 1) * 128], aT(k, tg * 512, (tg + 1) * 512),
                           k == 0, k == 7, [wkey, ("a", k, tg)], [("ps", b)])
                    evac_scaled(Qe[0:64, c, tg * 512:(tg + 1) * 512], PS[b][0:64, :], [("ps", b)],
                                [("Qe", c, tg)], 0.125)
                    evac_scaled(Qo[64:128, c, tg * 512:(tg + 1) * 512], PS[b][64:128, :], [("ps", b)],
                                [("Qo", c, tg)], 0.125)
            phase(3.03)
            for tg in range(4):
                b2 = genbank()
                for k in range(8):
                    mm(PS[b2][:, :], wkk[:, k, :],
                       aT(k, tg * 512, (tg + 1) * 512), k == 0, k == 7, [wkey, ("a", k, tg)], [("ps", b2)])
                evac_scaled(KD[:, tg * 512:(tg + 1) * 512], PS[b2][:, :], [("ps", b2)], [("KD", tg)], 1.0)
            phase(3.04)
            for ttg in range(2):
                b = genbank()
                pv = PS[b][:, :].rearrange("p (i c) -> p i c", i=8)
                for i in range(8):
                    tt = ttg * 8 + i
                    for k in range(8):
                        mm(pv[:, i, :], aT(k, tt * 128, (tt + 1) * 128), wvv[:, k, :], k == 0, k == 7,
                           [wkey, ("a", k, tt // 4)], [("ps", b)])
                evac_scaled(VS[:, ttg * 8:(ttg + 1) * 8, 0:64], pv, [("ps", b)], [("VS", ttg, 0)], 1.0)
                evac_scaled(VS[:, ttg * 8:(ttg + 1) * 8, 128:192], pv, [("ps", b)], [("VS", ttg, 1)], 1.0)
            BMv = BM.rearrange("p (t g x) -> p t g x", t=2, g=2)
            ch = 4 + 2 * g
            for n0 in range(0, 16, 2):
                phase(3.2)
                for jn in range(2):
                    n = n0 + jn
                    tiles = ([0] if n > 0 else []) + [1]
                    pi = sp_i[0] % 2
                    sp_i[0] += 1
                    pti = pt_i[0] % NPT
                    pt_i[0] += 1
                    pts = []
                    for ti, tl in enumerate(tiles):
                        kb = n - 1 if tl == 0 else n
                        sbk = 2 * pi + ti
                        mm(PS[sbk][:, :], IDB[:, :], BMv[:, tl, g, :], True, False, [("c", "id"), ("c", "bm")],
                           [("ps", sbk)])
                        mm(PS[sbk][:, 0:256].rearrange("p (c t) -> p c t", c=2), KD[:, kb * 128:(kb + 1) * 128],
                           Qe[:, :, n * 128:(n + 1) * 128], False, False,
                           [("KD", kb // 4), ("Qe", 0, n // 4), ("Qe", 1, n // 4), ("Qe", "pad")], [("ps", sbk)])
                        mm(PS[sbk][:, 256:512].rearrange("p (c t) -> p c t", c=2), KD[:, kb * 128:(kb + 1) * 128],
                           Qo[:, :, n * 128:(n + 1) * 128], False, True,
                           [("KD", kb // 4), ("Qo", 0, n // 4), ("Qo", 1, n // 4), ("Qo", "pad")], [("ps", sbk)])
                        pts.append((ti, kb))
                    nt = len(tiles)
                    act(PTP[pti][:, 0:nt, :], PSP[pi][:, 0:nt, :], AF.Exp,
                        [("ps", 2 * pi + ti) for ti in range(nt)], [("pt", pti)])
                    for ii, (ti, kb) in enumerate(pts):
                        mm(PS[ACCA][:, jn * 256:(jn + 1) * 256], VS[:, kb, 0:128], PTP[pti][:, ti, 0:256],
                           ii == 0, ii == len(pts) - 1,
                           [("pt", pti), ("VS", kb // 8, 0), ("VS", "ones")], [("ps", ACCA)])
                        mm(PS[ACCB][:, jn * 256:(jn + 1) * 256], VS[:, kb, 64:192], PTP[pti][:, ti, 256:512],
                           ii == 0, ii == len(pts) - 1,
                           [("pt", pti), ("VS", kb // 8, 1), ("VS", "ones")], [("ps", ACCB)])
                phase(3.3)
                di = (n0 // 2) % 2
                for c in range(2):
                    dve("tensor_scalar", [("ps", ACCA), ("c", "ske")], [("den", di, 1, c)],
                        DEN[di][64:128, :].rearrange("p (j c t) -> p j c t", j=2, c=2)[:, :, c, :],
                        PS[ACCA][64:128, :].rearrange("p (j c t) -> p j c t", j=2, c=2)[:, :, c, :],
                        SKE[64:128, g * 2 + c: g * 2 + c + 1], None, ALU.add)
                    dve("tensor_scalar", [("ps", ACCB), ("c", "ske")], [("den", di, 0, c)],
                        DEN[di][0:64, :].rearrange("p (j c t) -> p j c t", j=2, c=2)[:, :, c, :],
                        PS[ACCB][0:64, :].rearrange("p (j c t) -> p j c t", j=2, c=2)[:, :, c, :],
                        SKE[0:64, g * 2 + c: g * 2 + c + 1], None, ALU.add)
                swb = 2 * (sp_i[0] % 2)
                sp_i[0] += 1
                mm(PS[swb][:, :], SWAP[:, :], DEN[di][:, :], True, True,
                   [("den", di, a, c) for a in range(2) for c in range(2)] + [("c", "swap")], [("ps", swb)])
                act(RC[di][:, :], PS[swb][:, :], AF.Ln, [("ps", swb)], [("rc", di)])
                act(RC[di][:, :], RC[di][:, :], AF.Exp, [("rc", di)], [("rc", di)], scale=-1.0)
                tgk = n0 // 4
                dve("tensor_tensor", [("ps", ACCA), ("rc", di)], [("OT", ch, tgk, 0), ("OT", ch + 1, tgk, 0)],
                    OT[0:64, ch:ch + 2, n0 * 128:(n0 + 2) * 128].rearrange("p c (j t) -> p j c t", j=2),
                    PS[ACCA][0:64, :].rearrange("p (j c t) -> p j c t", j=2, c=2),
                    RC[di][0:64, :].rearrange("p (j c t) -> p j c t", j=2, c=2), ALU.mult)
                dve("tensor_tensor", [("ps", ACCB), ("rc", di)], [("OT", ch, tgk, 1), ("OT", ch + 1, tgk, 1)],
                    OT[64:128, ch:ch + 2, n0 * 128:(n0 + 2) * 128].rearrange("p c (j t) -> p j c t", j=2),
                    PS[ACCB][64:128, :].rearrange("p (j c t) -> p j c t", j=2, c=2),
                    RC[di][64:128, :].rearrange("p (j c t) -> p j c t", j=2, c=2), ALU.mult)

        phase(4)
        for p in range(4):
            hA, hB = 2 * p, 2 * p + 1
            pr.cur_buf = [0, 1, 0, 1][p]
            qE, qO, kE, kO, VF = fox_views(pr.cur_buf)
            if p < 2:
                pr.retire("B%d" % pr.cur_buf)
                const_fill(qE[64:96, :], 1.0, [], [("qE", "aug")])
                zero_fill(kE[64:96, :], [], [("kE", "aug")])
                const_fill(kE[64:67, :], -1.0, [("kE", "aug")], [("kE", "aug")])
                zero_fill(qO[0:64, :], [], [("qO", "aug")])
                const_fill(qO[0:32, :], 1.0, [("qO", "aug")], [("qO", "aug")])
                zero_fill(kO[0:64, :], [], [("kO", "aug")])
                const_fill(kO[0:3, :], -1.0, [("kO", "aug")], [("kO", "aug")])
                const_fill(VF[:, :, 64:128], 1.0, [], [("VF", "ones")])
            wt, wkey = wload([(0, 8, 128, wsrc(w_in, 0, D, 128 * p, 128 * p + 128)),
                              (1024, 8, 128, wsrc(w_in, 0, D, 512 + 128 * p, 512 + 128 * p + 128)),
                              (2048, 8, 128, wsrc(w_in, 0, D, 1024 + 128 * p, 1024 + 128 * p + 128))])
            wq = wview(wt, 0, 8, 128)
            wk = wview(wt, 1024, 8, 128)
            wvv = wview(wt, 2048, 8, 128)
            spdma(qE[64:67, :], augD[hA, :, :], [("augD",), ("qE", "aug")], [("qE", "aug")], "aug")
            spdma(qO[0:3, :], augD[hB, :, :], [("augD",), ("qO", "aug")], [("qO", "aug")], "aug")
            spdma(kE[67:70, :], augD[hA, :, :], [("augD",), ("kE", "aug")], [("kE", "aug")], "aug")
            spdma(kO[3:6, :], augD[hB, :, :], [("augD",), ("kO", "aug")], [("kO", "aug")], "aug")
            for tg in range(4):
                b = genbank()
                for k in range(8):
                    mm(PS[b][:, :], wq[:, k, :], aT(k, tg * 512, (tg + 1) * 512), k == 0, k == 7,
                       [wkey, ("a", k, tg)], [("ps", b)])
                evac_scaled(qE[0:64, tg * 512:(tg + 1) * 512], PS[b][0:64, :], [("ps", b)], [("qE", tg)], 0.125)
                evac_scaled(qO[64:128, tg * 512:(tg + 1) * 512], PS[b][64:128, :], [("ps", b)], [("qO", tg)], 0.125)
                b = genbank()
                for k in range(8):
                    mm(PS[b][:, :], wk[:, k, :], aT(k, tg * 512, (tg + 1) * 512), k == 0, k == 7,
                       [wkey, ("a", k, tg)], [("ps", b)])
                evac_scaled(kE[0:64, tg * 512:(tg + 1) * 512], PS[b][0:64, :], [("ps", b)], [("kE", tg)], 1.0)
                evac_scaled(kO[64:128, tg * 512:(tg + 1) * 512], PS[b][64:128, :], [("ps", b)], [("kO", tg)], 1.0)
            for ttg in range(4):
                b = genbank()
                pv = PS[b][:, :].rearrange("p (i c) -> p i c", i=4)
                for i in range(4):
                    tt = ttg * 4 + i
                    for k in range(8):
                        mm(pv[:, i, :], aT(k, tt * 128, (tt + 1) * 128), wvv[:, k, :], k == 0, k == 7,
                           [wkey, ("a", k, tt // 4)], [("ps", b)])
                evac_scaled(VF[:, ttg * 4:(ttg + 1) * 4, 0:64], pv[:, :, 0:64], [("ps", b)], [("VF", ttg, 0)], 1.0)
                evac_scaled(VF[:, ttg * 4:(ttg + 1) * 4, 128:192], pv[:, :, 64:128], [("ps", b)], [("VF", ttg, 1)], 1.0)

            for tg in range(4):
                nj = 4 * tg + 4
                def emit_PV(st, nj=nj):
                    j, pti, c0, N = st
                    mm(PS[ACCA][:, c0:512], VF[:, j, 0:128], PTP[pti][:, 0, 0:N], j == 0, j == nj - 1,
                       [("pt", pti), ("VF", j // 4, 0), ("VF", "ones")], [("ps", ACCA)])
                    mm(PS[ACCB][:, c0:512], VF[:, j, 64:192], PTP[pti][:, 1, 0:N], j == 0, j == nj - 1,
                       [("pt", pti), ("VF", j // 4, 1), ("VF", "ones")], [("ps", ACCB)])

                pend = []
                for j in range(nj):
                    r = j - 4 * tg
                    c0 = 128 * r if r >= 0 else 0
                    N = 512 - c0
                    pi = sp_i[0] % 2
                    sp_i[0] += 1
                    pti = pt_i[0] % NPT
                    pt_i[0] += 1
                    for hd in (0, 1):
                        sbk = 2 * pi + hd
                        if hd == 0:
                            qt, kt, K = qE, kE, 96
                            rk = [("qE", tg), ("qE", "aug"), ("kE", j // 4), ("kE", "aug")]
                        else:
                            qt, kt, K = qO, kO, 128
                            rk = [("qO", tg), ("qO", "aug"), ("kO", j // 4), ("kO", "aug")]
                        mm(PS[sbk][:, 0:N], kt[0:K, j * 128:(j + 1) * 128], qt[0:K, tg * 512 + c0:(tg + 1) * 512],
                           True, r < 0, rk, [("ps", sbk)])
                        if r >= 0:
                            mm(PS[sbk][:, 0:128], IDB[:, :], MFB[:, :], False, True, [("c", "id"), ("c", "mf")],
                               [("ps", sbk)])
                    act(PTP[pti][:, :, 0:N], PSP[pi][:, :, 0:N], AF.Exp, [("ps", 2 * pi), ("ps", 2 * pi + 1)],
                        [("pt", pti)])
                    pend.append((j, pti, c0, N))
                    if len(pend) > 1:
                        emit_PV(pend.pop(0))
                while pend:
                    emit_PV(pend.pop(0))
                di = (p * 4 + tg) % 2
                anyop([("ps", ACCA)], [("den", di, 1, 0), ("den", di, 1, 1)], DEN[di][64:128, :],
                      (DEN[di][64:128, :], PS[ACCA][64:128, :], AF.Copy), "tensor_copy",
                      (DEN[di][64:128, :], PS[ACCA][64:128, :]))
                anyop([("ps", ACCB)], [("den", di, 0, 0), ("den", di, 0, 1)], DEN[di][0:64, :],
                      (DEN[di][0:64, :], PS[ACCB][0:64, :], AF.Copy), "tensor_copy",
                      (DEN[di][0:64, :], PS[ACCB][0:64, :]))
                swb = 2 * (sp_i[0] % 2)
                sp_i[0] += 1
                mm(PS[swb][:, :], SWAP[:, :], DEN[di][:, :], True, True,
                   [("den", di, a, c) for a in range(2) for c in range(2)] + [("c", "swap")], [("ps", swb)])
                act(RC[di][:, :], PS[swb][:, :], AF.Ln, [("ps", swb)], [("rc", di)])
                act(RC[di][:, :], RC[di][:, :], AF.Exp, [("rc", di)], [("rc", di)], scale=-1.0)
                dve("tensor_tensor", [("ps", ACCA), ("rc", di)], [("OT", p, tg, 0)],
                    OT[0:64, p, tg * 512:(tg + 1) * 512], PS[ACCA][0:64, :], RC[di][0:64, :], ALU.mult)
                dve("tensor_tensor", [("ps", ACCB), ("rc", di)], [("OT", p, tg, 1)],
                    OT[64:128, p, tg * 512:(tg + 1) * 512], PS[ACCB][64:128, :], RC[di][64:128, :], ALU.mult)
        phase(4.9)
        pr.retire("B1")
        for k in range(4):
            spdma(hT[:, k, :], xT[k * 128:(k + 1) * 128, :], [], [("h", k, t) for t in range(4)], "xr%d" % k)
        pr.retire("X2")

        phase(5)
        GEN[:] = [6, 0, 1, 2, 3, 4, 5]
        wo_p = []
        for hh in range(2):
            wt, wkey = wload([(0, 8, 512, wsrc(w_out, 0, D, hh * 512, (hh + 1) * 512))])
            wo_p.append((wview(wt, 0, 8, 512), wkey))
        for tg in range(4):
            msb = 7
            for mc in range(8):
                b = genbank()
                for c in range(8):
                    wo, wkey = wo_p[mc // 4]
                    rk = [wkey, ("OT", c, tg, 0), ("OT", c, tg, 1)]
                    mm(PS[b][:, :], wo[:, c, (mc % 4) * 128:(mc % 4 + 1) * 128], OT[:, c, tg * 512:(tg + 1) * 512],
                       c == 0, c == 7, rk, [("ps", b)])
                anyop([("ps", b)], [("mix", mc)], MIX[:, mc, :], (MIX[:, mc, :], PS[b][:, :], AF.Copy), "tensor_copy",
                      (MIX[:, mc, :], PS[b][:, :]))
                stats_add(PS[b][:, :], [("ps", b)], msb, mc == 0, mc == 7)
            postnorm_update(1, tg, msb)
        pr.retire("X1")
        pr.retire("ZO_O")
        pr.retire("B0")
        pr.retire("BMR")
        pr.retire("DENR")
        pr.retire("RCR")

        phase(6)
        GEN[:] = [0, 1, 2, 3, 4, 6, 7]
        cur_pen[0] = 0.0
        for hf in range(2):
            for stg in range(2):
                tg = 2 * hf + stg
                prenorm(2, tg, lambda k, stg=stg: mT[:, k, stg * 512:(stg + 1) * 512],
                        lambda k, stg=stg: ("m", k, stg), alt=True)
            for wp in range(8):
                wt, wkey = wload([(0, 8, 512, wsrc(w_ff1, 0, D, wp * 512, (wp + 1) * 512))])
                w1 = wview(wt, 0, 8, 512)
                for fcl in range(4):
                    fc = wp * 4 + fcl
                    for stg in range(2):
                        b = genbank()
                        for k in range(8):
                            mm(PS[b][:, :], w1[:, k, fcl * 128:(fcl + 1) * 128], mT[:, k, stg * 512:(stg + 1) * 512],
                               k == 0, k == 7, [wkey, ("m", k, stg)], [("ps", b)])
                        ri = (fc * 2 + stg) % 2
                        anyop([("ps", b)], [("rt", ri)], RT[ri][:, :], (RT[ri][:, :], PS[b][:, :], AF.Relu),
                              "tensor_scalar", (RT[ri][:, :], PS[b][:, :], 0.0, None, ALU.max))
                        anyop([("rt", ri)], [("u", fc, stg)], U[:, fc, stg * 512:(stg + 1) * 512],
                              (U[:, fc, stg * 512:(stg + 1) * 512], RT[ri][:, :], AF.Square), "tensor_tensor",
                              (U[:, fc, stg * 512:(stg + 1) * 512], RT[ri][:, :], RT[ri][:, :], ALU.mult), fast=True)
            for stg in range(2):
                tg = 2 * hf + stg
                msb = 5
                for wp in range(4):
                    bks = [genbank(), genbank()]
                    for fh in range(2):
                        wt, wkey = wload([(0, 16, 256, wsrc(w_ff2, fh * 2048, 2048, wp * 256, (wp + 1) * 256))])
                        w2 = wview(wt, 0, 16, 256)
                        for mcl in range(2):
                            b = bks[mcl]
                            for f16 in range(16):
                                f = fh * 16 + f16
                                mm(PS[b][:, :], w2[:, f16, mcl * 128:(mcl + 1) * 128],
                                   U[:, f, stg * 512:(stg + 1) * 512], f == 0, f == 31,
                                   [wkey, ("u", f, stg)], [("ps", b)])
                    for mcl in range(2):
                        mc = 2 * wp + mcl
                        b = bks[mcl]
                        anyop([("ps", b)], [("mix", mc)], MIX[:, mc, :], (MIX[:, mc, :], PS[b][:, :], AF.Copy),
                              "tensor_copy", (MIX[:, mc, :], PS[b][:, :]))
                        stats_add(PS[b][:, :], [("ps", b)], msb, mc == 0, mc == 7)
                postnorm_update(3, tg, msb)

        phase(7)
        pr.retire("X1")
        pr.retire("RCR")
        wg_p = []
        for hh in range(2):
            wtg, wgkey = wload([(0, 8, 512, wsrc(w_gate, 0, D, hh * 512, (hh + 1) * 512))])
            wg_p.append((wview(wtg, 0, 8, 512), wgkey))
        wtp, wpkey = wload([(0, 2, 1024, wsrc(w_ple, 0, 256, 0, D))])
        wpl = wview(wtp, 0, 2, 1024)
        wtp2, wpkey2 = wload([(0, 2, 2048, pT[:, :].rearrange("(k p) t -> p k t", p=128))])
        pTb = wview(wtp2, 0, 2, 2048)
        yv = yT[:, :].rearrange("(k p) t -> p k t", p=128)
        pr.retire("ZO_O")
        GEN[:] = [0, 1, 2, 3, 6, 7]
        ykeys = []

        GROUPS = [(0, 0, 512, None), (1, 512, 512, None), (2, 1024, 512, None), (3, 1536, 256, "a"), (3, 1792, 256, "b")]

        def ple_bufs(gix):
            par = gix % 2
            return ((MIX, "mix") if par == 0 else (MIXB, "mixb")) + ((HB, "hb") if par == 0 else (HB2, "hb2")) + \
                   ((5,) if par == 0 else (4,))

        def ple_front(gix):
            tg, t0, n, sub = GROUPS[gix]
            mixt, mkey, HBt, hkey, msb = ple_bufs(gix)
            for k in range(8):
                act(HBt[:, k, 0:n], hT[:, k, t0:t0 + n], AF.Copy, [("h", k, tg)], [(hkey, k)])
            for mc in range(8):
                b = genbank()
                for k in range(8):
                    wg, wgkey = wg_p[mc // 4]
                    mm(PS[b][:, 0:n], wg[:, k, (mc % 4) * 128:(mc % 4 + 1) * 128], HBt[:, k, 0:n], k == 0, k == 7,
                       [wgkey, (hkey, k)], [("ps", b)])
                gi = mc % 2
                act(GT[gi][:, 0:n], PS[b][:, 0:n], AF.Tanh, [("ps", b)], [("gt", gi)], scale=0.5)
                b2 = genbank()
                for j in range(2):
                    mm(PS[b2][:, 0:n], wpl[:, j, mc * 128:(mc + 1) * 128], pTb[:, j, t0:t0 + n],
                       j == 0, j == 1, [wpkey, wpkey2], [("ps", b2)])
                dve("scalar_tensor_tensor", [("ps", b2), ("gt", gi)], [(mkey, mc), ("plm", mc)], mixt[:, mc, 0:n],
                    GT[gi][:, 0:n], 1.0, PS[b2][:, 0:n], ALU.add, ALU.mult)
                stats_add(mixt[:, mc, 0:n], [(mkey, mc)], msb, mc == 0, mc == 7, sbsrc=False, n=n)

        def ple_post(gix):
            tg, t0, n, sub = GROUPS[gix]
            mixt, mkey, HBt, hkey, msb = ple_bufs(gix)
            xr = lambda mc: [("plm", mc)]
            en = pr.enabled
            if sub is None:
                postnorm_update(4, tg, msb, mixt, mkey, eps=4.0 * EPS, xreads=xr)
                pr.enabled = True
                spdma(yv[:, :, t0:t0 + n], hT[:, :, t0:t0 + n],
                      [("h", k, tg) for k in range(8)], [("y", tg)], "out%d" % tg)
                ykeys.append(("y", tg))
            else:
                def out_mc(mc, tg=tg, t0=t0, n=n, sub=sub):
                    spdma(yT[mc * 128:(mc + 1) * 128, t0:t0 + n], hT[:, mc, t0:t0 + n],
                          [("h3", sub, mc)], [("y", tg, sub, mc)], "o3%s_%d" % (sub, mc))
                    ykeys.append(("y", tg, sub, mc))
                if pr.enabled:
                    postnorm_update(4, tg, msb, mixt, mkey, after=out_mc, eps=4.0 * EPS, xreads=xr, t0=t0, n=n,
                                    wkey=lambda mc, sub=sub: ("h3", sub, mc))
                elif sub == "a":
                    pr.enabled = True
                    for mc in range(8):
                        spdma(yT[mc * 128:(mc + 1) * 128, 1536:2048], hT[:, mc, 1536:2048],
                              [("h", mc, 3)], [("y", 3, "x", mc)], "o3a_%d" % mc)
                        ykeys.append(("y", 3, "x", mc))
            pr.enabled = en

        ple_front(0)
        ple_front(1)
        ple_post(0)
        ple_front(2)
        ple_post(1)
        ple_front(3)
        ple_post(2)
        ple_front(4)
        ple_post(3)
        ple_post(4)
        pr.enabled = True
        pr.add("sp", lambda e: None, reads=ykeys, writes=[])

        for op in pr.ops:
            op.setdefault("ndma", 1)
        pr.schedule()
        pr.emit(nc, stack)
    return nc


_CACHE = {}


def _t5_bucket(n):
    max_exact = 16
    large = max_exact + (np.log(np.maximum(n, 1) / max_exact) / np.log(128 / max_exact) * (32 - max_exact)).astype(np.int32)
    large = np.minimum(large, 31)
    return np.where(n < max_exact, n, large).astype(np.int32)


def kernel(x, p, w_in, b_forget, w_out, rel_bias, swa_sinks, g_attn_pre, g_attn_post,
           w_ff1, w_ff2, g_ff_pre, g_ff_post, w_ple, w_ple_gate, g_ple_post):
    f = lambda a: np.ascontiguousarray(np.asarray(a, dtype=np.float32))
    x = f(x); p = f(p)
    B = x.shape[0]
    if "nc" not in _CACHE:
        _CACHE["nc"] = build_program()
    nc = _CACHE["nc"]
    gs = np.stack([f(g_attn_pre)[0], f(g_attn_post)[0], f(g_ff_pre)[0], f(g_ff_post)[0], f(g_ple_post)[0]])
    gT = np.ascontiguousarray(gs.reshape(5, 8, 128).transpose(2, 0, 1).reshape(128, 40))
    rb = f(rel_bias)
    jj = np.arange(128)[:, None]
    ii = np.arange(128)[None, :]
    dist_prev = ii + 128 - jj
    dist_cur = ii - jj
    bk_prev = _t5_bucket(np.clip(dist_prev, 0, None))
    bk_cur = _t5_bucket(np.clip(dist_cur, 0, None))
    valid_prev = (dist_prev >= 0) & (dist_prev < 128)
    valid_cur = (dist_cur >= 0) & (dist_cur < 128)
    rb_ext = np.concatenate([rb, np.full((1, rb.shape[1]), NEG, np.float32)], axis=0)
    ix_prev = np.where(valid_prev, bk_prev, rb.shape[0])
    ix_cur = np.where(valid_cur, bk_cur, rb.shape[0])
    biasT = np.zeros((128, 2, 2, 4, 128), np.float32)
    for g in range(2):
        heads = [4 * g, 4 * g + 2, 4 * g + 1, 4 * g + 3]
        for sl, h in enumerate(heads):
            biasT[:, 0, g, sl, :] = rb_ext[ix_prev, h]
            biasT[:, 1, g, sl, :] = rb_ext[ix_cur, h]
    biasT = biasT.reshape(128, 2048)
    maskF = np.where(jj > ii, NEG, 0.0).astype(np.float32)
    ident = np.eye(128, dtype=np.float32)
    common = dict(w_in=f(w_in)[0], w_out=f(w_out)[0], w_ff1=f(w_ff1)[0], w_ff2=f(w_ff2)[0],
                  w_ple=f(w_ple)[0], w_gate=f(w_ple_gate)[0], gT=gT, bfg=f(b_forget).reshape(8, 1),
                  sinks=np.ascontiguousarray(np.tile(f(swa_sinks).reshape(1, 8), (128, 1))), biasT=biasT, maskF=maskF, ident=ident)
    in_maps = []
    for b in range(B):
        m = dict(common)
        m["xT"] = np.ascontiguousarray(x[b].T)
        m["pT"] = np.ascontiguousarray(p[0, b].T)
        in_maps.append(m)
    res = run_bass_kernel_spmd(nc, in_maps, core_ids=list(range(B)))
    out = np.stack([np.ascontiguousarray(res.results[b]["yT"].T) for b in range(B)]).astype(np.float32)
    return out
```

```python
import numpy as np
import concourse.bass as bass
import concourse.mybir as mybir
from concourse.bass_utils import run_bass_kernel_spmd

F32 = mybir.dt.float32
BF16 = mybir.dt.bfloat16
AF = mybir.ActivationFunctionType
ALU = mybir.AluOpType

S = 2048
D = 1024
NEG = -240000.0
EPS = 1e-6
ENGS = ["pe", "act", "dve", "pool", "sp"]


class Prog:
    def __init__(self):
        self.ops = []
        self.res = {}
        self.eng_ops = {e: [] for e in ENGS}
        self.dma_groups = {}
        self.barrier_deps = set()
        self.dma_since_barrier = []
        self.cur_buf = 0
        self.reg_members = {}
        self.reg_guard = {}

    UNITKEYS = ("Qe", "Qo", "KD", "VS", "qE", "qO", "kE", "kO", "VF")

    def km(self, k):
        if k[0] in self.UNITKEYS:
            return (k[0], self.cur_buf) + tuple(k[1:])
        return k

    @staticmethod
    def regions_of(k):
        n = k[0]
        if n == "a":
            return ("X1",) if k[1] < 4 else ("X2",)
        if n in ("m", "hb"):
            return ("X1",)
        if n == "mix":
            return ("X2",)
        if n == "tmp":
            return ("ZO_O",)
        if n in ("t1", "t2", "a3", "OT", "mixb", "hb2"):
            return ("ZO_O",)
        if n == "u":
            return ("ZO_O",) if k[1] < 16 else ("B0", "BMR")
        if n in Prog.UNITKEYS:
            return ("B%d" % k[1],)
        if n == "c" and k[1] == "bm":
            return ("BMR",)
        if n == "h" and k[1] < 4:
            return ("B1",)
        if n in ("den", "rt"):
            return ("DENR",)
        if n in ("rc", "gt", "rs2"):
            return ("RCR",)
        return ()

    def retire(self, R):
        mem = self.reg_members.get(R, [])
        if not mem or not self.enabled:
            return
        idx = len(self.ops)
        deps = set(mem)
        if self.reg_guard.get(R) is not None:
            deps.add(self.reg_guard[R])
        self.ops.append(dict(eng="sp", fn=lambda e: e.nop(), deps=deps, dma=None, sig=False, val=0,
                             cost=60.0, lat=0.0, alt=None, ndma=1))
        self.eng_ops["sp"].append(idx)
        self.reg_guard[R] = idx
        self.reg_members[R] = []

    enabled = True

    def add(self, eng, fn, reads=(), writes=(), dma=None, nobarrier=False, cost=300.0, lat=0.0, alt=None):
        if not self.enabled:
            self.ops_dummy = dict(ndma=1)
            return None
        idx = len(self.ops)
        reads = [self.km(k) for k in reads]
        writes = [self.km(k) for k in writes]
        ex = [k for k in reads if k[0] == "ps"]
        if ex:
            reads = [k for k in reads if k[0] != "ps"]
            writes = list(writes) + [k for k in ex if k not in writes]
        deps = set() if nobarrier else set(self.barrier_deps)
        for r in reads:
            st = self.res.get(r)
            if st is not None and st[0] is not None:
                deps.add(st[0])
        for w in writes:
            st = self.res.get(w)
            if st is not None:
                if st[0] is not None:
                    deps.add(st[0])
                deps.update(st[1])
        for r in reads:
            st = self.res.setdefault(r, [None, []])
            st[1].append(idx)
        for w in writes:
            self.res[w] = [idx, []]
        if dma is not None:
            g = self.dma_groups.setdefault(dma, [])
            if g:
                deps.add(g[-1])
            g.append(idx)
            if not nobarrier:
                self.dma_since_barrier.append(idx)
        for k in list(reads) + list(writes):
            for R in self.regions_of(k):
                g = self.reg_guard.get(R)
                if g is not None:
                    deps.add(g)
                self.reg_members.setdefault(R, []).append(idx)
        deps.discard(idx)
        self.ops.append(dict(eng=eng, fn=fn, deps=deps, dma=dma, sig=False, val=0, cost=cost, lat=lat, alt=alt))
        self.eng_ops[eng].append(idx)
        return idx

    def schedule(self):
        ops = self.ops
        n = len(ops)
        ndeps = [len(op["deps"]) for op in ops]
        users = [[] for _ in range(n)]
        for i, op in enumerate(ops):
            for d in op["deps"]:
                users[d].append(i)
        rt = [0.0] * n
        fin = [0.0] * n
        free = {e: 0.0 for e in ENGS}
        order = {e: [] for e in ENGS}
        avail = [i for i in range(n) if ndeps[i] == 0]
        import os
        SYNC = float(os.environ.get("K_SYNC", "120"))
        WIN = float(os.environ.get("K_WIN", "0"))
        PRIO = os.environ.get("K_PRIO", "bl")
        bl = [0.0] * n
        for i in range(n - 1, -1, -1):
            m = 0.0
            for u in users[i]:
                if bl[u] > m:
                    m = bl[u]
            bl[i] = ops[i]["cost"] + ops[i]["lat"] + m
        dma_free = 0.0
        import os
        ACT_PEN = float(os.environ.get("ACT_PEN", "300"))
        done = 0
        while avail:
            cands = []
            mins = None
            for i in avail:
                op = ops[i]
                best = None
                opts = [(op["eng"], op["cost"], None)]
                if op["alt"] is not None:
                    opts.append((op["alt"][0], op["alt"][2], op["alt"]))
                bestv = None
                for (e, c, a) in opts:
                    st = max(rt[i], free[e])
                    v = st + c + (op.get("pen", 0.0) if (len(opts) > 1 and e == "act") else 0.0)
                    if best is None or v < bestv:
                        best = (st, e, c, a)
                        bestv = v
                cands.append((best[0], i, best))
                if mins is None or best[0] < mins:
                    mins = best[0]
            pick = None
            for (st, i, b) in cands:
                if st <= mins + WIN:
                    if pick is None:
                        pick = (st, i, b)
                    elif PRIO == "bl":
                        if bl[i] > bl[pick[1]]:
                            pick = (st, i, b)
                    elif i < pick[1]:
                        pick = (st, i, b)
            st, i, (st_, e, c, a) = pick
            op = ops[i]
            if a is not None:
                op["eng"] = a[0]
                op["fn"] = a[1]
                op["cost"] = a[2]
            order[e].append(i)
            free[e] = st + c
            if op["dma"] is not None:
                d0 = max(st + c, dma_free)
                dma_free = d0 + op.get("nbytes", 0) / 330.0
                fin[i] = dma_free + 2000.0
            else:
                fin[i] = st + c + op["lat"]
            avail.remove(i)
            for u in users[i]:
                ndeps[u] -= 1
                same_pe = (e == "pe" and ops[u]["eng"] == "pe" and ops[u]["alt"] is None)
                t = fin[i] + (0.0 if same_pe else SYNC)
                if t > rt[u]:
                    rt[u] = t
                if ndeps[u] == 0:
                    avail.append(u)
            done += 1
        assert done == n, (done, n)
        self.eng_ops = order
        self.makespan = max(fin) if fin else 0.0
        busy = {e: sum(ops[i]["cost"] for i in order[e]) for e in ENGS}
        print("[sched] est makespan %.1f us; busy us: %s" % (self.makespan / 1e3, {e: round(busy[e] / 1e3, 1) for e in ENGS}))

    def barrier(self):
        deps = set()
        for e in ["pe", "act", "dve", "sp"]:
            if self.eng_ops[e]:
                deps.add(self.eng_ops[e][-1])
        deps.update(self.dma_since_barrier)
        self.dma_since_barrier = []
        self.barrier_deps = deps

    def emit(self, nc, stack):
        ops = self.ops
        waited = set()
        for op in ops:
            for d in op["deps"]:
                if ops[d]["dma"] is None and ops[d]["eng"] == "pe" and op["eng"] == "pe":
                    continue
                waited.add(d)
        cnt = {e: 0 for e in ENGS}
        dcnt = {}
        for e in ENGS:
            for i in self.eng_ops[e]:
                op = ops[i]
                assert op["eng"] == e
                if op["dma"] is None:
                    if i in waited:
                        cnt[e] += 1
                        op["sig"] = True
                    op["val"] = cnt[e]
        engsem = {e: stack.enter_context(nc.semaphore("s_" + e)) for e in ENGS}
        dmasem = {g: stack.enter_context(nc.semaphore("d_%d" % i)) for i, g in enumerate(self.dma_groups)}
        dmaval = {g: 0 for g in self.dma_groups}
        block = stack.enter_context(nc.Block())

        def run(ename, eng):
            seen = {}
            for i in self.eng_ops[ename]:
                op = ops[i]
                for d in sorted(op["deps"]):
                    dop = ops[d]
                    if dop["dma"] is not None:
                        key = ("dma", dop["dma"])
                        sem = dmasem[dop["dma"]]
                    else:
                        if dop["eng"] == "pe" and ename == "pe":
                            continue
                        key = ("eng", dop["eng"])
                        sem = engsem[dop["eng"]]
                    v = dop["val"]
                    assert v > 0, (d, i)
                    if seen.get(key, 0) >= v:
                        continue
                    eng.wait_ge(sem, v)
                    seen[key] = v
                r = op["fn"](eng)
                if op["dma"] is not None:
                    insts = r if isinstance(r, (list, tuple)) else [r]
                    for ins in insts:
                        ins.then_inc(dmasem[op["dma"]], 16)
                elif op["sig"]:
                    r.then_inc(engsem[ename], 1)

        for i, op in enumerate(ops):
            if op["dma"] is not None:
                dmaval[op["dma"]] += 16 * op["ndma"]
                op["val"] = dmaval[op["dma"]]

        @block.tensor
        def _(e):
            run("pe", e)

        @block.scalar
        def _(e):
            run("act", e)

        @block.vector
        def _(e):
            run("dve", e)

        @block.gpsimd
        def _(e):
            run("pool", e)

        @block.sync
        def _(e):
            run("sp", e)


def build_program(stop=99):
    nc = bass.Bass("TRN2", target_bir_lowering=False)
    pr = Prog()

    def dram(name, shape, dt=F32, kind="ExternalInput"):
        return nc.dram_tensor(name, shape, dt, kind=kind).ap()

    xT = dram("xT", [D, S])
    pT = dram("pT", [256, S])
    w_in = dram("w_in", [D, 2312])
    w_out = dram("w_out", [D, D])
    w_ff1 = dram("w_ff1", [D, 4096])
    w_ff2 = dram("w_ff2", [4096, D])
    w_ple = dram("w_ple", [256, D])
    w_gate = dram("w_gate", [D, D])
    gT = dram("gT", [128, 40])
    bfg = dram("bfg", [8, 1])
    sinks = dram("sinks", [128, 8])
    biasT = dram("biasT", [128, 2048])
    maskF = dram("maskF", [128, 128])
    identD = dram("ident", [128, 128])
    yT = dram("yT", [D, S], kind="ExternalOutput")
    augD = dram("augscr", [8, 3, S], BF16, kind="Internal")

    from contextlib import ExitStack

    stack = ExitStack()
    with stack:
        def sb(name, shape, dt):
            return stack.enter_context(nc.sbuf_tensor(name, shape, dt))

        hT = sb("hT", [128, 8, S], F32)
        X1 = sb("X1", [128, 8192], BF16)
        X2 = sb("X2", [128, 8192], BF16)
        ZO = sb("ZO", [128, 32768], BF16)
        NWS = 4
        WS = [sb("WS%d" % i, [128, 4096], BF16) for i in range(NWS)]
        G = sb("G", [128, 40], F32)
        IDB = sb("IDB", [128, 128], BF16)
        MFB = sb("MFB", [128, 128], BF16)
        ONESB = sb("ONESB", [128, 128], BF16)
        SWAP = sb("SWAP", [128, 128], BF16)
        SKE = sb("SKE", [128, 4], F32)
        SK = sb("SK", [128, 8], F32)
        BFt = sb("BFt", [8, 1], F32)
        NBt = sb("NBt", [8, 1], F32)
        NRS = 1
        NSQ = 2
        NPT = 2
        RS = [sb("RS%d" % i, [128, 512], F32) for i in range(NRS)]
        SQ = [sb("SQ%d" % i, [128, 512], BF16) for i in range(NSQ)]
        PTP = [sb("PT%d" % i, [128, 2, 512], BF16) for i in range(NPT)]
        RE = sb("RE", [128, 512], BF16)
        RO = sb("RO", [128, 512], BF16)
        RT = [RE, RO]
        DEN = [RE, RO]
        DT = sb("DT", [128, 512], F32)
        BCSt = sb("BCS", [128, 1024], BF16)
        RC = [DT, BCSt.bitcast(F32)]
        GT = RC
        PSa = stack.enter_context(nc.psum_tensor("psa", [128, 1024], F32))
        PSb = stack.enter_context(nc.psum_tensor("psb", [128, 1024], F32))
        PS = [PSa[:, 0:512], PSa[:, 512:1024], PSb[:, 0:512], PSb[:, 512:1024]]
        PS += [stack.enter_context(nc.psum_tensor("ps%d" % i, [128, 512], F32))[:, :] for i in range(4, 8)]
        PSP = [PSa[:, :].rearrange("p (a x) -> p a x", a=2), PSb[:, :].rearrange("p (a x) -> p a x", a=2)]
        ACCA, ACCB = 4, 5

        X2f = X2.bitcast(F32)
        ZOf = ZO.bitcast(F32)
        MIX = X2f[:, 0:4096].rearrange("p (m t) -> p m t", m=8)

        def aT(k, c0, c1):
            t = X1 if k < 4 else X2
            kk = k % 4
            return t[:, kk * 2048 + c0: kk * 2048 + c1]

        OT = ZO[:, 0:16384].rearrange("p (c t) -> p c t", c=8)
        ZB = 16384
        BM = ZO[:, ZB + 13312: ZB + 15360]
        hTb = hT.bitcast(BF16)[:, :, :].rearrange("p k t -> p (k t)")
        UB = [ZO[:, ZB: ZB + 13312], hTb[:, 0:13312]]

        def swa_views(b):
            B_ = UB[b]
            return (B_[:, 0:4096].rearrange("p (c t) -> p c t", c=2),
                    B_[:, 4096:8192].rearrange("p (c t) -> p c t", c=2),
                    B_[:, 8192:10240],
                    B_[:, 10240:13312].rearrange("p (t c) -> p t c", t=16))

        def fox_views(b):
            B_ = UB[b]
            return (B_[:, 0:2048], B_[:, 2048:4096], B_[:, 4096:6144], B_[:, 6144:8192],
                    B_[:, 8192:11264].rearrange("p (t c) -> p t c", t=16))
        T1 = ZOf[0:8, 0:2048]
        T2 = ZOf[0:8, 2048:4096]
        A3 = ZO[0:8, 8192: 8192 + 6144].rearrange("p (r t) -> p r t", r=3)
        mT = X1[:, 0:8192].rearrange("p (k t) -> p k t", k=8)
        U = ZO[:, 0:32768].rearrange("p (f t) -> p f t", f=32)
        HB = X1[:, 0:4096].rearrange("p (k t) -> p k t", k=8)
        MIXB = ZOf[:, 0:4096].rearrange("p (m t) -> p m t", m=8)
        HB2 = ZO[:, 8192:12288].rearrange("p (k t) -> p k t", k=8)

        def phase(n):
            pr.enabled = n <= stop

        def nfree(ap):
            n = 1
            for d in list(ap.shape)[1:]:
                n *= int(d)
            return n

        def mm(out, lhsT, rhs, start, stop, reads, writes):
            c = max(nfree(rhs), 64) / 2.4 + 8.0
            return pr.add("pe", lambda e: e.matmul(out, lhsT, rhs, start=start, stop=stop,
                                                   skip_group_check=True), reads, writes, cost=c)

        def act_fn(out, in_, func, bias=None, scale=None):
            kw = {}
            if bias is not None:
                kw["bias"] = bias
            if scale is not None:
                kw["scale"] = scale
            return lambda e: e.activation(out, in_, func, **kw)

        def act_cost(out):
            return 210.0 + 0.833 * nfree(out)

        def act(out, in_, func, reads, writes, bias=None, scale=None):
            return pr.add("act", act_fn(out, in_, func, bias, scale), reads, writes, cost=act_cost(out))

        def dve_fn(fnname, *args, **kw):
            return lambda e: getattr(e, fnname)(*args, **kw)

        def dve_cost(out, fast=False):
            return 90.0 + nfree(out) * (0.52 if fast else 1.04)

        def dve(fnname, reads, writes, *args, **kw):
            fast = kw.pop("fast", False)
            return pr.add("dve", dve_fn(fnname, *args, **kw), reads, writes, cost=dve_cost(args[0], fast))

        cur_pen = [0.0]

        def anyop(reads, writes, out, act_args, dve_name, dve_args, fast=False):
            i = pr.add("act", act_fn(*act_args), reads, writes, cost=act_cost(out),
                       alt=("dve", dve_fn(dve_name, *dve_args), dve_cost(out, fast)))
            if i is not None:
                pr.ops[i]["pen"] = cur_pen[0]
            return i

        U32 = mybir.dt.uint32
        BITS = {1.0: 0x3F803F80, -1.0: 0xBF80BF80}

        def zero_fill(ap, reads, writes):
            n2 = nfree(ap) // 2
            i = pr.add("act", (lambda e: e.memzero(ap)), reads, writes, cost=160.0 + 0.833 * n2,
                       alt=("dve", (lambda e: e.memzero(ap)), 140.0 + 1.04 * n2))
            if i is not None:
                pr.ops[i]["pen"] = 0.0
            return i

        def const_fill(ap, val, reads, writes):
            apu = ap.bitcast(U32)
            return pr.add("dve", (lambda e: e.memset(apu, BITS[val])), reads, writes,
                          cost=140.0 + 1.04 * (nfree(ap) // 2))

        wcount = [0]

        def wload(dmas):
            s = wcount[0] % NWS
            wcount[0] += 1
            t = WS[s]

            def fn(e):
                r = []
                for ent in dmas:
                    if len(ent) == 2:
                        dst = ent[0](t)
                        src = ent[1]
                    else:
                        (c0, K, C, src) = ent
                        dst = t[:, c0: c0 + K * C].rearrange("p (k c) -> p k c", k=K)
                    r.append(e.dma_start(out=dst, in_=src))
                return r
            nb = 0
            for ent in dmas:
                nb += nfree(ent[1]) * 128 * 4 if len(ent) == 2 else ent[1] * ent[2] * 128 * 4
            rd = [("h", 0, 3)] if wcount[0] in (3, 4) else []
            i = pr.add("pool", fn, reads=rd, writes=[("ws", s)], dma="w%d" % s, nobarrier=True,
                       cost=1200.0 * len(dmas), lat=2500.0 + nb / 300.0)
            if i is not None:
                pr.ops[i]["nbytes"] = nb
            if i is not None:
                pr.ops[i]["ndma"] = len(dmas)
            return t, ("ws", s)

        def spdma(out, in_, reads, writes, group):
            i = pr.add("sp", lambda e: e.dma_start(out=out, in_=in_), reads, writes, dma=group,
                       cost=150.0, lat=2200.0 + nfree(out) * 128 * 4 / 200.0)
            if i is not None:
                pr.ops[i]["ndma"] = 1
                pr.ops[i]["nbytes"] = nfree(out) * int(out.shape[0]) * 4
            return i

        def wview(t, c0, K, C):
            return t[:, c0: c0 + K * C].rearrange("p (k c) -> p k c", k=K)

        def wsrc(w, r0, nrows, c0, c1):
            return w[r0: r0 + nrows, c0:c1].rearrange("(k p) c -> p k c", p=128)

        gen_i = [0]
        GEN = [6, 7]

        def genbank():
            b = GEN[gen_i[0] % len(GEN)]
            gen_i[0] += 1
            return b

        sq_i = [0]

        def sqtile():
            i = sq_i[0] % NSQ
            sq_i[0] += 1
            return i

        rs_i = [0]

        SQPOOL = [(SQ[i][:, :], ("sq", i)) for i in range(NSQ)]
        sqp_i = [0]

        def stats_add(src, src_reads, msb, first, last, sbsrc=False, n=512):
            sqt, sqk = SQPOOL[sqp_i[0] % len(SQPOOL)]
            sqt = sqt[:, 0:n]
            sqp_i[0] += 1
            if sbsrc:
                anyop(src_reads, [sqk], sqt, (sqt, src, AF.Square), "tensor_tensor", (sqt, src, src, ALU.mult))
            else:
                act(sqt, src, AF.Square, src_reads, [sqk])
            mm(PS[msb][:, 0:n], ONESB[:, :], sqt, first, last, [sqk, ("c", "ones")], [("ps", msb)])

        def rstd_from(msb, alt=False, eps=EPS, n=512):
            if alt:
                t, key = DT, ("rs2",)
            else:
                i = rs_i[0] % NRS
                rs_i[0] += 1
                t, key = RS[i], ("rs", i)
            dve("tensor_scalar", [("ps", msb)], [key], t[:, 0:n], PS[msb][:, 0:n], 1.0 / D, eps,
                ALU.mult, ALU.add)
            act(t[:, 0:n], t[:, 0:n], AF.Ln, [key], [key])
            act(t[:, 0:n], t[:, 0:n], AF.Exp, [key], [key], scale=-0.5)
            return t, key

        def prenorm(gidx, tg, dst_fn, dst_keys, alt=False):
            msb = genbank()
            for k in range(8):
                stats_add(hT[:, k, tg * 512:(tg + 1) * 512], [("h", k, tg), ("pno", k)], msb, k == 0, k == 7,
                          sbsrc=True)
            rt_, rk_ = rstd_from(msb, alt)
            for k in range(8):
                dve("scalar_tensor_tensor", [("h", k, tg), rk_, ("c", "g")], [dst_keys(k), ("pno", k)],
                    dst_fn(k), hT[:, k, tg * 512:(tg + 1) * 512], G[:, gidx * 8 + k: gidx * 8 + k + 1],
                    rt_[:, :], ALU.mult, ALU.mult)

        def postnorm_update(gidx, tg, msb, mixt=None, mkey="mix", after=None, eps=EPS, xreads=None,
                            t0=None, n=512, wkey=None):
            if mixt is None:
                mixt = MIX
            if t0 is None:
                t0 = tg * 512
            rt_, rk_ = rstd_from(msb, eps=eps, n=n)
            for mc in range(8):
                dve("scalar_tensor_tensor", [(mkey, mc), rk_, ("c", "g")] + (xreads(mc) if xreads else []),
                    [(mkey, mc)],
                    mixt[:, mc, 0:n], mixt[:, mc, 0:n], G[:, gidx * 8 + mc: gidx * 8 + mc + 1], rt_[:, 0:n],
                    ALU.mult, ALU.mult)
                wk = [("h", mc, tg)] if wkey is None else [wkey(mc)]
                dve("tensor_tensor", [(mkey, mc), ("h", mc, tg)], wk,
                    hT[:, mc, t0:t0 + n], hT[:, mc, t0:t0 + n], mixt[:, mc, 0:n], ALU.add)
                if after is not None:
                    after(mc)

        phase(0)
        xv = xT[:, :].rearrange("(k p) t -> p k t", p=128)
        def xload(t):
            o_ = hT[:, :, t * 512:(t + 1) * 512]
            i_ = xv[:, :, t * 512:(t + 1) * 512]
            i = pr.add("pool", (lambda e: e.dma_start(out=o_, in_=i_)), [], [("h", k, t) for k in range(8)],
                       dma="x%d" % t, cost=1200.0, lat=2500.0)
            if i is not None:
                pr.ops[i]["ndma"] = 1
                pr.ops[i]["nbytes"] = 2 * 1024 * 1024
        spdma(G[:, :], gT[:, :], [], [("c", "g")], "c0")
        xload(0)
        spdma(BFt[:, :], bfg[:, :], [], [("c", "bf")], "c1")
        spdma(SK[:, :], sinks[:, :], [], [("c", "sk")], "c2")
        xload(1)
        xload(2)
        xload(3)
        spdma(ZOf[:, 0:128], identD[:, :], [], [("tmp", "id")], "c3")
        spdma(ZOf[:, 128:256], maskF[:, :], [], [("tmp", "mf")], "c3")
        i_bm = pr.add("pool", (lambda e: e.dma_start(out=BM, in_=biasT[:, :])), [], [("c", "bm")], dma="bm",
                      cost=1200.0, lat=2500.0)
        if i_bm is not None:
            pr.ops[i_bm]["ndma"] = 1
            pr.ops[i_bm]["nbytes"] = 1024 * 1024

        dve("memset", [], [("c", "ones")], ONESB[:, :], 1.0)

        phase(1)
        SQPOOL.extend([(RE[:, :], ("den", 0, "s", 0)), (RO[:, :], ("den", 1, "s", 0)),
                       (BCSt[:, 0:512], ("rc", "s", 0)), (BCSt[:, 512:1024], ("rc", "s", 1))])
        for tg in range(4):
            prenorm(0, tg, lambda k, tg=tg: aT(k, tg * 512, (tg + 1) * 512), lambda k, tg=tg: ("a", k, tg))
        del SQPOOL[NSQ:]
        pr.retire("DENR")
        pr.retire("RCR")
        LATE = [("a", 7, 3)]
        dve("tensor_copy", [("tmp", "id")] + LATE, [("c", "id")], IDB[:, :], ZOf[:, 0:128])
        dve("tensor_copy", [("tmp", "mf")] + LATE, [("c", "mf")], MFB[:, :], ZOf[:, 128:256])
        dve("tensor_copy", [("c", "id")], [("c", "swap")], SWAP[:, 0:64], IDB[:, 64:128])
        dve("tensor_copy", [("c", "id"), ("c", "swap")], [("c", "swap")], SWAP[:, 64:128], IDB[:, 0:64])
        dve("tensor_scalar", [("c", "bf")] + LATE, [("c", "nb")], NBt[:, :], BFt[:, :], -1.0, None, ALU.mult)
        act(SK[:, :], SK[:, :], AF.Exp, [("c", "sk")] + LATE, [("c", "sk")])
        for g in range(2):
            for c in range(2):
                he = 4 * g + 2 * c
                dve("tensor_copy", [("c", "sk"), ("c", "ske")], [("c", "ske")], SKE[64:128, g * 2 + c: g * 2 + c + 1],
                    SK[64:128, he:he + 1])
                dve("tensor_copy", [("c", "sk"), ("c", "ske")], [("c", "ske")], SKE[0:64, g * 2 + c: g * 2 + c + 1],
                    SK[0:64, he + 1:he + 2])
        pr.retire("ZO_O")

        a_all = [("a", k, t) for k in range(8) for t in range(4)]

        phase(2)
        wt, wkey = wload([(0, 8, 128, wsrc(w_in, 0, D, 1536, 1664))])
        wv = wview(wt, 0, 8, 128)
        for tg in range(4):
            b = genbank()
            for k in range(8):
                mm(PS[b][:, :], wv[:, k, :], aT(k, tg * 512, (tg + 1) * 512), k == 0, k == 7,
                   [wkey, ("a", k, tg)], [("ps", b)])
            act(T1[:, tg * 512:(tg + 1) * 512], PS[b][0:8, :], AF.Exp, [("ps", b), ("c", "nb")], [("t1",)],
                bias=NBt[:, 0:1], scale=-1.0)
        act(T1, T1, AF.Ln, [("t1",)], [("t1",)], bias=1.0, scale=1.0)
        dve("tensor_tensor_scan", [("t1",)], [("t2",)], T2, T1, T1, 0.0, ALU.add, ALU.max)
        dve("tensor_copy", [("t2",)], [("a3", 0)], A3[:, 0, :], T2)
        dve("tensor_tensor", [("t2",), ("a3", 0)], [("t1",)], T1, T2, A3[:, 0, :], ALU.subtract)
        dve("tensor_copy", [("t1",)], [("a3", 1)], A3[:, 1, :], T1)
        dve("tensor_tensor", [("t1",), ("a3", 1)], [("t2",)], T2, T1, A3[:, 1, :], ALU.subtract)
        dve("tensor_copy", [("t2",)], [("a3", 2)], A3[:, 2, :], T2)
        spdma(augD[:, :, :], A3, [("a3", 0), ("a3", 1), ("a3", 2)], [("augD",)], "aug")
        pr.retire("ZO_O")

        phase(3)
        cur_pen[0] = 300.0
        sbanks = [0, 1, 2]
        s_i = [0]
        sp_i = [0]
        pt_i = [0]
        ev_i = [0]

        def evac_scaled(out, in_, reads, writes, scale):
            anyop(reads, writes, out, (out, in_, AF.Copy, None, scale), "tensor_scalar",
                  (out, in_, scale, None, ALU.mult))

        for g in range(2):
            phase(3)
            pr.cur_buf = [0, 1][g]
            if g == 1:
                pr.retire("B1")
            Qe, Qo, KD, VS = swa_views(pr.cur_buf)
            zero_fill(Qe[64:128, :, :], [], [("Qe", "pad")])
            zero_fill(Qo[0:64, :, :], [], [("Qo", "pad")])
            const_fill(VS[:, :, 64:128], 1.0, [], [("VS", "ones")])
            qc0 = 1544 + 256 * g
            kc0 = 2056 + 64 * g
            vc0 = 2184 + 64 * g
            wt, wkey = wload([(0, 8, 256, wsrc(w_in, 0, D, qc0, qc0 + 256)),
                              (lambda t: t[:, 2048:3072].rearrange("p (k c) -> p k c", k=8)[:, :, 0:64],
                               wsrc(w_in, 0, D, kc0, kc0 + 64)),
                              (lambda t: t[:, 2048:3072].rearrange("p (k c) -> p k c", k=8)[:, :, 64:128],
                               wsrc(w_in, 0, D, kc0, kc0 + 64)),
                              (3072, 8, 64, wsrc(w_in, 0, D, vc0, vc0 + 64))])
            wq = wview(wt, 0, 8, 256)
            wkk = wview(wt, 2048, 8, 128)
            wvv = wview(wt, 3072, 8, 64)
            phase(3.02)
            for c in range(2):
                for tg in range(4):
                    b = genbank()
                    for k in range(8):
                        mm(PS[b][:, :], wq[:, k, c * 128:(c + 1) * 128], aT(k, tg * 512, (tg + 1) * 512),
                           k == 0, k == 7, [wkey, ("a", k, tg)], [("ps", b)])
                    evac_scaled(Qe[0:64, c, tg * 512:(tg + 1) * 512], PS[b][0:64, :], [("ps", b)],
                                [("Qe", c, tg)], 0.125)
                    evac_scaled(Qo[64:128, c, tg * 512:(tg + 1) * 512], PS[b][64:128, :], [("ps", b)],
                                [("Qo", c, tg)], 0.125)
            phase(3.03)
            for tg in range(4):
                b2 = genbank()
                for k in range(8):
                    mm(PS[b2][:, :], wkk[:, k, :],
                       aT(k, tg * 512, (tg + 1) * 512), k == 0, k == 7, [wkey, ("a", k, tg)], [("ps", b2)])
                evac_scaled(KD[:, tg * 512:(tg + 1) * 512], PS[b2][:, :], [("ps", b2)], [("KD", tg)], 1.0)
            phase(3.04)
            for ttg in range(2):
                b = genbank()
                pv = PS[b][:, :].rearrange("p (i c) -> p i c", i=8)
                for i in range(8):
                    tt = ttg * 8 + i
                    for k in range(8):
                        mm(pv[:, i, :], aT(k, tt * 128, (tt + 1) * 128), wvv[:, k, :], k == 0, k == 7,
                           [wkey, ("a", k, tt // 4)], [("ps", b)])
                evac_scaled(VS[:, ttg * 8:(ttg + 1) * 8, 0:64], pv, [("ps", b)], [("VS", ttg, 0)], 1.0)
                evac_scaled(VS[:, ttg * 8:(ttg + 1) * 8, 128:192], pv, [("ps", b)], [("VS", ttg, 1)], 1.0)
            BMv = BM.rearrange("p (t g x) -> p t g x", t=2, g=2)
            ch = 4 + 2 * g
            for n0 in range(0, 16, 2):
                phase(3.2)
                for jn in range(2):
                    n = n0 + jn
                    tiles = ([0] if n > 0 else []) + [1]
                    pi = sp_i[0] % 2
                    sp_i[0] += 1
                    pti = pt_i[0] % NPT
                    pt_i[0] += 1
                    pts = []
                    for ti, tl in enumerate(tiles):
                        kb = n - 1 if tl == 0 else n
                        sbk = 2 * pi + ti
                        mm(PS[sbk][:, :], IDB[:, :], BMv[:, tl, g, :], True, False, [("c", "id"), ("c", "bm")],
                           [("ps", sbk)])
                        mm(PS[sbk][:, 0:256].rearrange("p (c t) -> p c t", c=2), KD[:, kb * 128:(kb + 1) * 128],
                           Qe[:, :, n * 128:(n + 1) * 128], False, False,
                           [("KD", kb // 4), ("Qe", 0, n // 4), ("Qe", 1, n // 4), ("Qe", "pad")], [("ps", sbk)])
                        mm(PS[sbk][:, 256:512].rearrange("p (c t) -> p c t", c=2), KD[:, kb * 128:(kb + 1) * 128],
                           Qo[:, :, n * 128:(n + 1) * 128], False, True,
                           [("KD", kb // 4), ("Qo", 0, n // 4), ("Qo", 1, n // 4), ("Qo", "pad")], [("ps", sbk)])
                        pts.append((ti, kb))
                    nt = len(tiles)
                    act(PTP[pti][:, 0:nt, :], PSP[pi][:, 0:nt, :], AF.Exp,
                        [("ps", 2 * pi + ti) for ti in range(nt)], [("pt", pti)])
                    for ii, (ti, kb) in enumerate(pts):
                        mm(PS[ACCA][:, jn * 256:(jn + 1) * 256], VS[:, kb, 0:128], PTP[pti][:, ti, 0:256],
                           ii == 0, ii == len(pts) - 1,
                           [("pt", pti), ("VS", kb // 8, 0), ("VS", "ones")], [("ps", ACCA)])
                        mm(PS[ACCB][:, jn * 256:(jn + 1) * 256], VS[:, kb, 64:192], PTP[pti][:, ti, 256:512],
                           ii == 0, ii == len(pts) - 1,
                           [("pt", pti), ("VS", kb // 8, 1), ("VS", "ones")], [("ps", ACCB)])
                phase(3.3)
                di = (n0 // 2) % 2
                for c in range(2):
                    dve("tensor_scalar", [("ps", ACCA), ("c", "ske")], [("den", di, 1, c)],
                        DEN[di][64:128, :].rearrange("p (j c t) -> p j c t", j=2, c=2)[:, :, c, :],
                        PS[ACCA][64:128, :].rearrange("p (j c t) -> p j c t", j=2, c=2)[:, :, c, :],
                        SKE[64:128, g * 2 + c: g * 2 + c + 1], None, ALU.add)
                    dve("tensor_scalar", [("ps", ACCB), ("c", "ske")], [("den", di, 0, c)],
                        DEN[di][0:64, :].rearrange("p (j c t) -> p j c t", j=2, c=2)[:, :, c, :],
                        PS[ACCB][0:64, :].rearrange("p (j c t) -> p j c t", j=2, c=2)[:, :, c, :],
                        SKE[0:64, g * 2 + c: g * 2 + c + 1], None, ALU.add)
                swb = 2 * (sp_i[0] % 2)
                sp_i[0] += 1
                mm(PS[swb][:, :], SWAP[:, :], DEN[di][:, :], True, True,
                   [("den", di, a, c) for a in range(2) for c in range(2)] + [("c", "swap")], [("ps", swb)])
                act(RC[di][:, :], PS[swb][:, :], AF.Ln, [("ps", swb)], [("rc", di)])
                act(RC[di][:, :], RC[di][:, :], AF.Exp, [("rc", di)], [("rc", di)], scale=-1.0)
                tgk = n0 // 4
                dve("tensor_tensor", [("ps", ACCA), ("rc", di)], [("OT", ch, tgk, 0), ("OT", ch + 1, tgk, 0)],
                    OT[0:64, ch:ch + 2, n0 * 128:(n0 + 2) * 128].rearrange("p c (j t) -> p j c t", j=2),
                    PS[ACCA][0:64, :].rearrange("p (j c t) -> p j c t", j=2, c=2),
                    RC[di][0:64, :].rearrange("p (j c t) -> p j c t", j=2, c=2), ALU.mult)
                dve("tensor_tensor", [("ps", ACCB), ("rc", di)], [("OT", ch, tgk, 1), ("OT", ch + 1, tgk, 1)],
                    OT[64:128, ch:ch + 2, n0 * 128:(n0 + 2) * 128].rearrange("p c (j t) -> p j c t", j=2),
                    PS[ACCB][64:128, :].rearrange("p (j c t) -> p j c t", j=2, c=2),
                    RC[di][64:128, :].rearrange("p (j c t) -> p j c t", j=2, c=2), ALU.mult)

        phase(4)
        for p in range(4):
            hA, hB = 2 * p, 2 * p + 1
            pr.cur_buf = [0, 1, 0, 1][p]
            qE, qO, kE, kO, VF = fox_views(pr.cur_buf)
            if p < 2:
                pr.retire("B%d" % pr.cur_buf)
                const_fill(qE[64:96, :], 1.0, [], [("qE", "aug")])
                zero_fill(kE[64:96, :], [], [("kE", "aug")])
                const_fill(kE[64:67, :], -1.0, [("kE", "aug")], [("kE", "aug")])
                zero_fill(qO[0:64, :], [], [("qO", "aug")])
                const_fill(qO[0:32, :], 1.0, [("qO", "aug")], [("qO", "aug")])
                zero_fill(kO[0:64, :], [], [("kO", "aug")])
                const_fill(kO[0:3, :], -1.0, [("kO", "aug")], [("kO", "aug")])
                const_fill(VF[:, :, 64:128], 1.0, [], [("VF", "ones")])
            wt, wkey = wload([(0, 8, 128, wsrc(w_in, 0, D, 128 * p, 128 * p + 128)),
                              (1024, 8, 128, wsrc(w_in, 0, D, 512 + 128 * p, 512 + 128 * p + 128)),
                              (2048, 8, 128, wsrc(w_in, 0, D, 1024 + 128 * p, 1024 + 128 * p + 128))])
            wq = wview(wt, 0, 8, 128)
            wk = wview(wt, 1024, 8, 128)
            wvv = wview(wt, 2048, 8, 128)
            spdma(qE[64:67, :], augD[hA, :, :], [("augD",), ("qE", "aug")], [("qE", "aug")], "aug")
            spdma(qO[0:3, :], augD[hB, :, :], [("augD",), ("qO", "aug")], [("qO", "aug")], "aug")
            spdma(kE[67:70, :], augD[hA, :, :], [("augD",), ("kE", "aug")], [("kE", "aug")], "aug")
            spdma(kO[3:6, :], augD[hB, :, :], [("augD",), ("kO", "aug")], [("kO", "aug")], "aug")
            for tg in range(4):
                b = genbank()
                for k in range(8):
                    mm(PS[b][:, :], wq[:, k, :], aT(k, tg * 512, (tg + 1) * 512), k == 0, k == 7,
                       [wkey, ("a", k, tg)], [("ps", b)])
                evac_scaled(qE[0:64, tg * 512:(tg + 1) * 512], PS[b][0:64, :], [("ps", b)], [("qE", tg)], 0.125)
                evac_scaled(qO[64:128, tg * 512:(tg + 1) * 512], PS[b][64:128, :], [("ps", b)], [("qO", tg)], 0.125)
                b = genbank()
                for k in range(8):
                    mm(PS[b][:, :], wk[:, k, :], aT(k, tg * 512, (tg + 1) * 512), k == 0, k == 7,
                       [wkey, ("a", k, tg)], [("ps", b)])
                evac_scaled(kE[0:64, tg * 512:(tg + 1) * 512], PS[b][0:64, :], [("ps", b)], [("kE", tg)], 1.0)
                evac_scaled(kO[64:128, tg * 512:(tg + 1) * 512], PS[b][64:128, :], [("ps", b)], [("kO", tg)], 1.0)
            for ttg in range(4):
                b = genbank()
                pv = PS[b][:, :].rearrange("p (i c) -> p i c", i=4)
                for i in range(4):
                    tt = ttg * 4 + i
                    for k in range(8):
                        mm(pv[:, i, :], aT(k, tt * 128, (tt + 1) * 128), wvv[:, k, :], k == 0, k == 7,
                           [wkey, ("a", k, tt // 4)], [("ps", b)])
                evac_scaled(VF[:, ttg * 4:(ttg + 1) * 4, 0:64], pv[:, :, 0:64], [("ps", b)], [("VF", ttg, 0)], 1.0)
                evac_scaled(VF[:, ttg * 4:(ttg + 1) * 4, 128:192], pv[:, :, 64:128], [("ps", b)], [("VF", ttg, 1)], 1.0)

            for tg in range(4):
                nj = 4 * tg + 4
                def emit_PV(st, nj=nj):
                    j, pti, c0, N = st
                    mm(PS[ACCA][:, c0:512], VF[:, j, 0:128], PTP[pti][:, 0, 0:N], j == 0, j == nj - 1,
                       [("pt", pti), ("VF", j // 4, 0), ("VF", "ones")], [("ps", ACCA)])
                    mm(PS[ACCB][:, c0:512], VF[:, j, 64:192], PTP[pti][:, 1, 0:N], j == 0, j == nj - 1,
                       [("pt", pti), ("VF", j // 4, 1), ("VF", "ones")], [("ps", ACCB)])

                pend = []
                for j in range(nj):
                    r = j - 4 * tg
                    c0 = 128 * r if r >= 0 else 0
                    N = 512 - c0
                    pi = sp_i[0] % 2
                    sp_i[0] += 1
                    pti = pt_i[0] % NPT
                    pt_i[0] += 1
                    for hd in (0, 1):
                        sbk = 2 * pi + hd
                        if hd == 0:
                            qt, kt, K = qE, kE, 96
                            rk = [("qE", tg), ("qE", "aug"), ("kE", j // 4), ("kE", "aug")]
                        else:
                            qt, kt, K = qO, kO, 128
                            rk = [("qO", tg), ("qO", "aug"), ("kO", j // 4), ("kO", "aug")]
                        mm(PS[sbk][:, 0:N], kt[0:K, j * 128:(j + 1) * 128], qt[0:K, tg * 512 + c0:(tg + 1) * 512],
                           True, r < 0, rk, [("ps", sbk)])
                        if r >= 0:
                            mm(PS[sbk][:, 0:128], IDB[:, :], MFB[:, :], False, True, [("c", "id"), ("c", "mf")],
                               [("ps", sbk)])
                    act(PTP[pti][:, :, 0:N], PSP[pi][:, :, 0:N], AF.Exp, [("ps", 2 * pi), ("ps", 2 * pi + 1)],
                        [("pt", pti)])
                    pend.append((j, pti, c0, N))
                    if len(pend) > 1:
                        emit_PV(pend.pop(0))
                while pend:
                    emit_PV(pend.pop(0))
                di = (p * 4 + tg) % 2
                anyop([("ps", ACCA)], [("den", di, 1, 0), ("den", di, 1, 1)], DEN[di][64:128, :],
                      (DEN[di][64:128, :], PS[ACCA][64:128, :], AF.Copy), "tensor_copy",
                      (DEN[di][64:128, :], PS[ACCA][64:128, :]))
                anyop([("ps", ACCB)], [("den", di, 0, 0), ("den", di, 0, 1)], DEN[di][0:64, :],
                      (DEN[di][0:64, :], PS[ACCB][0:64, :], AF.Copy), "tensor_copy",
                      (DEN[di][0:64, :], PS[ACCB][0:64, :]))
                swb = 2 * (sp_i[0] % 2)
                sp_i[0] += 1
                mm(PS[swb][:, :], SWAP[:, :], DEN[di][:, :], True, True,
                   [("den", di, a, c) for a in range(2) for c in range(2)] + [("c", "swap")], [("ps", swb)])
                act(RC[di][:, :], PS[swb][:, :], AF.Ln, [("ps", swb)], [("rc", di)])
                act(RC[di][:, :], RC[di][:, :], AF.Exp, [("rc", di)], [("rc", di)], scale=-1.0)
                dve("tensor_tensor", [("ps", ACCA), ("rc", di)], [("OT", p, tg, 0)],
                    OT[0:64, p, tg * 512:(tg + 1) * 512], PS[ACCA][0:64, :], RC[di][0:64, :], ALU.mult)
                dve("tensor_tensor", [("ps", ACCB), ("rc", di)], [("OT", p, tg, 1)],
                    OT[64:128, p, tg * 512:(tg + 1) * 512], PS[ACCB][64:128, :], RC[di][64:128, :], ALU.mult)
        phase(4.9)
        pr.retire("B1")
        for k in range(4):
            spdma(hT[:, k, :], xT[k * 128:(k + 1) * 128, :], [], [("h", k, t) for t in range(4)], "xr%d" % k)
        pr.retire("X2")

        phase(5)
        GEN[:] = [6, 0, 1, 2, 3, 4, 5]
        wo_p = []
        for hh in range(2):
            wt, wkey = wload([(0, 8, 512, wsrc(w_out, 0, D, hh * 512, (hh + 1) * 512))])
            wo_p.append((wview(wt, 0, 8, 512), wkey))
        for tg in range(4):
            msb = 7
            for mc in range(8):
                b = genbank()
                for c in range(8):
                    wo, wkey = wo_p[mc // 4]
                    rk = [wkey, ("OT", c, tg, 0), ("OT", c, tg, 1)]
                    mm(PS[b][:, :], wo[:, c, (mc % 4) * 128:(mc % 4 + 1) * 128], OT[:, c, tg * 512:(tg + 1) * 512],
                       c == 0, c == 7, rk, [("ps", b)])
                anyop([("ps", b)], [("mix", mc)], MIX[:, mc, :], (MIX[:, mc, :], PS[b][:, :], AF.Copy), "tensor_copy",
                      (MIX[:, mc, :], PS[b][:, :]))
                stats_add(PS[b][:, :], [("ps", b)], msb, mc == 0, mc == 7)
            postnorm_update(1, tg, msb)
        pr.retire("X1")
        pr.retire("ZO_O")
        pr.retire("B0")
        pr.retire("BMR")
        pr.retire("DENR")
        pr.retire("RCR")

        phase(6)
        GEN[:] = [0, 1, 2, 3, 4, 6, 7]
        cur_pen[0] = 0.0
        for hf in range(2):
            for stg in range(2):
                tg = 2 * hf + stg
                prenorm(2, tg, lambda k, stg=stg: mT[:, k, stg * 512:(stg + 1) * 512],
                        lambda k, stg=stg: ("m", k, stg), alt=True)
            for wp in range(8):
                wt, wkey = wload([(0, 8, 512, wsrc(w_ff1, 0, D, wp * 512, (wp + 1) * 512))])
                w1 = wview(wt, 0, 8, 512)
                for fcl in range(4):
                    fc = wp * 4 + fcl
                    for stg in range(2):
                        b = genbank()
                        for k in range(8):
                            mm(PS[b][:, :], w1[:, k, fcl * 128:(fcl + 1) * 128], mT[:, k, stg * 512:(stg + 1) * 512],
                               k == 0, k == 7, [wkey, ("m", k, stg)], [("ps", b)])
                        ri = (fc * 2 + stg) % 2
                        anyop([("ps", b)], [("rt", ri)], RT[ri][:, :], (RT[ri][:, :], PS[b][:, :], AF.Relu),
                              "tensor_scalar", (RT[ri][:, :], PS[b][:, :], 0.0, None, ALU.max))
                        anyop([("rt", ri)], [("u", fc, stg)], U[:, fc, stg * 512:(stg + 1) * 512],
                              (U[:, fc, stg * 512:(stg + 1) * 512], RT[ri][:, :], AF.Square), "tensor_tensor",
                              (U[:, fc, stg * 512:(stg + 1) * 512], RT[ri][:, :], RT[ri][:, :], ALU.mult), fast=True)
            for stg in range(2):
                tg = 2 * hf + stg
                msb = 5
                for wp in range(4):
                    bks = [genbank(), genbank()]
                    for fh in range(2):
                        wt, wkey = wload([(0, 16, 256, wsrc(w_ff2, fh * 2048, 2048, wp * 256, (wp + 1) * 256))])
                        w2 = wview(wt, 0, 16, 256)
                        for mcl in range(2):
                            b = bks[mcl]
                            for f16 in range(16):
                                f = fh * 16 + f16
                                mm(PS[b][:, :], w2[:, f16, mcl * 128:(mcl + 1) * 128],
                                   U[:, f, stg * 512:(stg + 1) * 512], f == 0, f == 31,
                                   [wkey, ("u", f, stg)], [("ps", b)])
                    for mcl in range(2):
                        mc = 2 * wp + mcl
                        b = bks[mcl]
                        anyop([("ps", b)], [("mix", mc)], MIX[:, mc, :], (MIX[:, mc, :], PS[b][:, :], AF.Copy),
                              "tensor_copy", (MIX[:, mc, :], PS[b][:, :]))
                        stats_add(PS[b][:, :], [("ps", b)], msb, mc == 0, mc == 7)
                postnorm_update(3, tg, msb)

        phase(7)
        pr.retire("X1")
        pr.retire("RCR")
        wg_p = []
        for hh in range(2):
            wtg, wgkey = wload([(0, 8, 512, wsrc(w_gate, 0, D, hh * 512, (hh + 1) * 512))])
            wg_p.append((wview(wtg, 0, 8, 512), wgkey))
        wtp, wpkey = wload([(0, 2, 1024, wsrc(w_ple, 0, 256, 0, D))])
        wpl = wview(wtp, 0, 2, 1024)
        wtp2, wpkey2 = wload([(0, 2, 2048, pT[:, :].rearrange("(k p) t -> p k t", p=128))])
        pTb = wview(wtp2, 0, 2, 2048)
        yv = yT[:, :].rearrange("(k p) t -> p k t", p=128)
        pr.retire("ZO_O")
        GEN[:] = [0, 1, 2, 3, 6, 7]
        ykeys = []

        GROUPS = [(0, 0, 512, None), (1, 512, 512, None), (2, 1024, 512, None), (3, 1536, 256, "a"), (3, 1792, 256, "b")]

        def ple_bufs(gix):
            par = gix % 2
            return ((MIX, "mix") if par == 0 else (MIXB, "mixb")) + ((HB, "hb") if par == 0 else (HB2, "hb2")) + \
                   ((5,) if par == 0 else (4,))

        def ple_front(gix):
            tg, t0, n, sub = GROUPS[gix]
            mixt, mkey, HBt, hkey, msb = ple_bufs(gix)
            for k in range(8):
                act(HBt[:, k, 0:n], hT[:, k, t0:t0 + n], AF.Copy, [("h", k, tg)], [(hkey, k)])
            for mc in range(8):
                b = genbank()
                for k in range(8):
                    wg, wgkey = wg_p[mc // 4]
                    mm(PS[b][:, 0:n], wg[:, k, (mc % 4) * 128:(mc % 4 + 1) * 128], HBt[:, k, 0:n], k == 0, k == 7,
                       [wgkey, (hkey, k)], [("ps", b)])
                gi = mc % 2
                act(GT[gi][:, 0:n], PS[b][:, 0:n], AF.Tanh, [("ps", b)], [("gt", gi)], scale=0.5)
                b2 = genbank()
                for j in range(2):
                    mm(PS[b2][:, 0:n], wpl[:, j, mc * 128:(mc + 1) * 128], pTb[:, j, t0:t0 + n],
                       j == 0, j == 1, [wpkey, wpkey2], [("ps", b2)])
                dve("scalar_tensor_tensor", [("ps", b2), ("gt", gi)], [(mkey, mc), ("plm", mc)], mixt[:, mc, 0:n],
                    GT[gi][:, 0:n], 1.0, PS[b2][:, 0:n], ALU.add, ALU.mult)
                stats_add(mixt[:, mc, 0:n], [(mkey, mc)], msb, mc == 0, mc == 7, sbsrc=False, n=n)

        def ple_post(gix):
            tg, t0, n, sub = GROUPS[gix]
            mixt, mkey, HBt, hkey, msb = ple_bufs(gix)
            xr = lambda mc: [("plm", mc)]
            en = pr.enabled
            if sub is None:
                postnorm_update(4, tg, msb, mixt, mkey, eps=4.0 * EPS, xreads=xr)
                pr.enabled = True
                o_ = yv[:, :, t0:t0 + n]
                i_ = hT[:, :, t0:t0 + n]
                i_o = pr.add("pool", (lambda e, o_=o_, i_=i_: e.dma_start(out=o_, in_=i_)),
                             [("h", k, tg) for k in range(8)], [("y", tg)], dma="out%d" % tg, cost=1200.0, lat=2500.0)
                if i_o is not None:
                    pr.ops[i_o]["ndma"] = 1
                    pr.ops[i_o]["nbytes"] = 2 * 1024 * 1024
                ykeys.append(("y", tg))
            else:
                def out_mc(mc, tg=tg, t0=t0, n=n, sub=sub):
                    spdma(yT[mc * 128:(mc + 1) * 128, t0:t0 + n], hT[:, mc, t0:t0 + n],
                          [("h3", sub, mc)], [("y", tg, sub, mc)], "o3%s_%d" % (sub, mc))
                    ykeys.append(("y", tg, sub, mc))
                if pr.enabled:
                    postnorm_update(4, tg, msb, mixt, mkey, after=out_mc, eps=4.0 * EPS, xreads=xr, t0=t0, n=n,
                                    wkey=lambda mc, sub=sub: ("h3", sub, mc))
                elif sub == "a":
                    pr.enabled = True
                    for mc in range(8):
                        spdma(yT[mc * 128:(mc + 1) * 128, 1536:2048], hT[:, mc, 1536:2048],
                              [("h", mc, 3)], [("y", 3, "x", mc)], "o3a_%d" % mc)
                        ykeys.append(("y", 3, "x", mc))
            pr.enabled = en

        ple_front(0)
        ple_front(1)
        ple_post(0)
        ple_front(2)
        ple_post(1)
        ple_front(3)
        ple_post(2)
        ple_front(4)
        ple_post(3)
        ple_post(4)
        pr.enabled = True
        pr.add("sp", lambda e: None, reads=ykeys, writes=[])

        for op in pr.ops:
            op.setdefault("ndma", 1)
        pr.schedule()
        pr.emit(nc, stack)
    return nc


_CACHE = {}


def _t5_bucket(n):
    max_exact = 16
    large = max_exact + (np.log(np.maximum(n, 1) / max_exact) / np.log(128 / max_exact) * (32 - max_exact)).astype(np.int32)
    large = np.minimum(large, 31)
    return np.where(n < max_exact, n, large).astype(np.int32)


def kernel(x, p, w_in, b_forget, w_out, rel_bias, swa_sinks, g_attn_pre, g_attn_post,
           w_ff1, w_ff2, g_ff_pre, g_ff_post, w_ple, w_ple_gate, g_ple_post):
    f = lambda a: np.ascontiguousarray(np.asarray(a, dtype=np.float32))
    x = f(x); p = f(p)
    B = x.shape[0]
    if "nc" not in _CACHE:
        _CACHE["nc"] = build_program()
    nc = _CACHE["nc"]
    gs = np.stack([f(g_attn_pre)[0], f(g_attn_post)[0], f(g_ff_pre)[0], f(g_ff_post)[0], f(g_ple_post)[0]])
    gT = np.ascontiguousarray(gs.reshape(5, 8, 128).transpose(2, 0, 1).reshape(128, 40))
    rb = f(rel_bias)
    jj = np.arange(128)[:, None]
    ii = np.arange(128)[None, :]
    dist_prev = ii + 128 - jj
    dist_cur = ii - jj
    bk_prev = _t5_bucket(np.clip(dist_prev, 0, None))
    bk_cur = _t5_bucket(np.clip(dist_cur, 0, None))
    valid_prev = (dist_prev >= 0) & (dist_prev < 128)
    valid_cur = (dist_cur >= 0) & (dist_cur < 128)
    rb_ext = np.concatenate([rb, np.full((1, rb.shape[1]), NEG, np.float32)], axis=0)
    ix_prev = np.where(valid_prev, bk_prev, rb.shape[0])
    ix_cur = np.where(valid_cur, bk_cur, rb.shape[0])
    biasT = np.zeros((128, 2, 2, 4, 128), np.float32)
    for g in range(2):
        heads = [4 * g, 4 * g + 2, 4 * g + 1, 4 * g + 3]
        for sl, h in enumerate(heads):
            biasT[:, 0, g, sl, :] = rb_ext[ix_prev, h]
            biasT[:, 1, g, sl, :] = rb_ext[ix_cur, h]
    biasT = biasT.reshape(128, 2048)
    maskF = np.where(jj > ii, NEG, 0.0).astype(np.float32)
    ident = np.eye(128, dtype=np.float32)
    common = dict(w_in=f(w_in)[0], w_out=f(w_out)[0], w_ff1=f(w_ff1)[0], w_ff2=f(w_ff2)[0],
                  w_ple=f(w_ple)[0], w_gate=f(w_ple_gate)[0], gT=gT, bfg=f(b_forget).reshape(8, 1),
                  sinks=np.ascontiguousarray(np.tile(f(swa_sinks).reshape(1, 8), (128, 1))), biasT=biasT, maskF=maskF, ident=ident)
    in_maps = []
    for b in range(B):
        m = dict(common)
        m["xT"] = np.ascontiguousarray(x[b].T)
        m["pT"] = np.ascontiguousarray(p[0, b].T)
        in_maps.append(m)
    res = run_bass_kernel_spmd(nc, in_maps, core_ids=list(range(B)))
    out = np.stack([np.ascontiguousarray(res.results[b]["yT"].T) for b in range(B)]).astype(np.float32)
    return out
```

```python
import numpy as np
import concourse.bass as bass
import concourse.mybir as mybir
from concourse.bass_utils import run_bass_kernel_spmd

F32 = mybir.dt.float32
BF16 = mybir.dt.bfloat16
AF = mybir.ActivationFunctionType
ALU = mybir.AluOpType

S = 2048
D = 1024
NEG = -240000.0
EPS = 1e-6
ENGS = ["pe", "act", "dve", "pool", "sp"]


class Prog:
    def __init__(self):
        self.ops = []
        self.res = {}
        self.eng_ops = {e: [] for e in ENGS}
        self.dma_groups = {}
        self.barrier_deps = set()
        self.dma_since_barrier = []
        self.cur_buf = 0
        self.reg_members = {}
        self.reg_guard = {}

    UNITKEYS = ("Qe", "Qo", "KD", "VS", "qE", "qO", "kE", "kO", "VF")

    def km(self, k):
        if k[0] in self.UNITKEYS:
            return (k[0], self.cur_buf) + tuple(k[1:])
        return k

    @staticmethod
    def regions_of(k):
        n = k[0]
        if n == "a":
            return ("X1",) if k[1] < 4 else ("X2",)
        if n in ("m", "hb"):
            return ("X1",)
        if n == "mix":
            return ("X2",)
        if n == "tmp":
            return ("ZO_O",)
        if n in ("t1", "t2", "a3", "OT", "mixb", "hb2"):
            return ("ZO_O",)
        if n == "u":
            return ("ZO_O",) if k[1] < 16 else ("B0", "BMR")
        if n in Prog.UNITKEYS:
            return ("B%d" % k[1],)
        if n == "c" and k[1] == "bm":
            return ("BMR",)
        if n == "h" and k[1] < 4:
            return ("B1",)
        if n in ("den", "rt"):
            return ("DENR",)
        if n in ("rc", "gt", "rs2"):
            return ("RCR",)
        return ()

    def retire(self, R):
        mem = self.reg_members.get(R, [])
        if not mem or not self.enabled:
            return
        idx = len(self.ops)
        deps = set(mem)
        if self.reg_guard.get(R) is not None:
            deps.add(self.reg_guard[R])
        self.ops.append(dict(eng="sp", fn=lambda e: e.nop(), deps=deps, dma=None, sig=False, val=0,
                             cost=60.0, lat=0.0, alt=None, ndma=1))
        self.eng_ops["sp"].append(idx)
        self.reg_guard[R] = idx
        self.reg_members[R] = []

    enabled = True

    def add(self, eng, fn, reads=(), writes=(), dma=None, nobarrier=False, cost=300.0, lat=0.0, alt=None):
        if not self.enabled:
            self.ops_dummy = dict(ndma=1)
            return None
        idx = len(self.ops)
        reads = [self.km(k) for k in reads]
        writes = [self.km(k) for k in writes]
        ex = [k for k in reads if k[0] == "ps"]
        if ex:
            reads = [k for k in reads if k[0] != "ps"]
            writes = list(writes) + [k for k in ex if k not in writes]
        deps = set() if nobarrier else set(self.barrier_deps)
        for r in reads:
            st = self.res.get(r)
            if st is not None and st[0] is not None:
                deps.add(st[0])
        for w in writes:
            st = self.res.get(w)
            if st is not None:
                if st[0] is not None:
                    deps.add(st[0])
                deps.update(st[1])
        for r in reads:
            st = self.res.setdefault(r, [None, []])
            st[1].append(idx)
        for w in writes:
            self.res[w] = [idx, []]
        if dma is not None:
            g = self.dma_groups.setdefault(dma, [])
            if g:
                deps.add(g[-1])
            g.append(idx)
            if not nobarrier:
                self.dma_since_barrier.append(idx)
        for k in list(reads) + list(writes):
            for R in self.regions_of(k):
                g = self.reg_guard.get(R)
                if g is not None:
                    deps.add(g)
                self.reg_members.setdefault(R, []).append(idx)
        deps.discard(idx)
        self.ops.append(dict(eng=eng, fn=fn, deps=deps, dma=dma, sig=False, val=0, cost=cost, lat=lat, alt=alt))
        self.eng_ops[eng].append(idx)
        return idx

    def schedule(self):
        ops = self.ops
        n = len(ops)
        ndeps = [len(op["deps"]) for op in ops]
        users = [[] for _ in range(n)]
        for i, op in enumerate(ops):
            for d in op["deps"]:
                users[d].append(i)
        rt = [0.0] * n
        fin = [0.0] * n
        free = {e: 0.0 for e in ENGS}
        order = {e: [] for e in ENGS}
        avail = [i for i in range(n) if ndeps[i] == 0]
        import os
        SYNC = float(os.environ.get("K_SYNC", "120"))
        WIN = float(os.environ.get("K_WIN", "0"))
        PRIO = os.environ.get("K_PRIO", "bl")
        bl = [0.0] * n
        for i in range(n - 1, -1, -1):
            m = 0.0
            for u in users[i]:
                if bl[u] > m:
                    m = bl[u]
            bl[i] = ops[i]["cost"] + ops[i]["lat"] + m
        dma_free = 0.0
        import os
        ACT_PEN = float(os.environ.get("ACT_PEN", "300"))
        done = 0
        while avail:
            cands = []
            mins = None
            for i in avail:
                op = ops[i]
                best = None
                opts = [(op["eng"], op["cost"], None)]
                if op["alt"] is not None:
                    opts.append((op["alt"][0], op["alt"][2], op["alt"]))
                bestv = None
                for (e, c, a) in opts:
                    st = max(rt[i], free[e])
                    v = st + c + (op.get("pen", 0.0) if (len(opts) > 1 and e == "act") else 0.0)
                    if best is None or v < bestv:
                        best = (st, e, c, a)
                        bestv = v
                cands.append((best[0], i, best))
                if mins is None or best[0] < mins:
                    mins = best[0]
            pick = None
            for (st, i, b) in cands:
                if st <= mins + WIN:
                    if pick is None:
                        pick = (st, i, b)
                    elif PRIO == "bl":
                        if bl[i] > bl[pick[1]]:
                            pick = (st, i, b)
                    elif i < pick[1]:
                        pick = (st, i, b)
            st, i, (st_, e, c, a) = pick
            op = ops[i]
            if a is not None:
                op["eng"] = a[0]
                op["fn"] = a[1]
                op["cost"] = a[2]
            order[e].append(i)
            free[e] = st + c
            if op["dma"] is not None:
                d0 = max(st + c, dma_free)
                dma_free = d0 + op.get("nbytes", 0) / 330.0
                fin[i] = dma_free + 2000.0
            else:
                fin[i] = st + c + op["lat"]
            avail.remove(i)
            for u in users[i]:
                ndeps[u] -= 1
                same_pe = (e == "pe" and ops[u]["eng"] == "pe" and ops[u]["alt"] is None)
                t = fin[i] + (0.0 if same_pe else SYNC)
                if t > rt[u]:
                    rt[u] = t
                if ndeps[u] == 0:
                    avail.append(u)
            done += 1
        assert done == n, (done, n)
        self.eng_ops = order
        self.makespan = max(fin) if fin else 0.0
        busy = {e: sum(ops[i]["cost"] for i in order[e]) for e in ENGS}
        print("[sched] est makespan %.1f us; busy us: %s" % (self.makespan / 1e3, {e: round(busy[e] / 1e3, 1) for e in ENGS}))

    def barrier(self):
        deps = set()
        for e in ["pe", "act", "dve", "sp"]:
            if self.eng_ops[e]:
                deps.add(self.eng_ops[e][-1])
        deps.update(self.dma_since_barrier)
        self.dma_since_barrier = []
        self.barrier_deps = deps

    def emit(self, nc, stack):
        ops = self.ops
        waited = set()
        for op in ops:
            for d in op["deps"]:
                if ops[d]["dma"] is None and ops[d]["eng"] == "pe" and op["eng"] == "pe":
                    continue
                waited.add(d)
        cnt = {e: 0 for e in ENGS}
        dcnt = {}
        for e in ENGS:
            for i in self.eng_ops[e]:
                op = ops[i]
                assert op["eng"] == e
                if op["dma"] is None:
                    if i in waited:
                        cnt[e] += 1
                        op["sig"] = True
                    op["val"] = cnt[e]
        engsem = {e: stack.enter_context(nc.semaphore("s_" + e)) for e in ENGS}
        dmasem = {g: stack.enter_context(nc.semaphore("d_%d" % i)) for i, g in enumerate(self.dma_groups)}
        dmaval = {g: 0 for g in self.dma_groups}
        block = stack.enter_context(nc.Block())

        def run(ename, eng):
            seen = {}
            for i in self.eng_ops[ename]:
                op = ops[i]
                for d in sorted(op["deps"]):
                    dop = ops[d]
                    if dop["dma"] is not None:
                        key = ("dma", dop["dma"])
                        sem = dmasem[dop["dma"]]
                    else:
                        if dop["eng"] == "pe" and ename == "pe":
                            continue
                        key = ("eng", dop["eng"])
                        sem = engsem[dop["eng"]]
                    v = dop["val"]
                    assert v > 0, (d, i)
                    if seen.get(key, 0) >= v:
                        continue
                    eng.wait_ge(sem, v)
                    seen[key] = v
                r = op["fn"](eng)
                if op["dma"] is not None:
                    insts = r if isinstance(r, (list, tuple)) else [r]
                    for ins in insts:
                        ins.then_inc(dmasem[op["dma"]], 16)
                elif op["sig"]:
                    r.then_inc(engsem[ename], 1)

        for i, op in enumerate(ops):
            if op["dma"] is not None:
                dmaval[op["dma"]] += 16 * op["ndma"]
                op["val"] = dmaval[op["dma"]]

        @block.tensor
        def _(e):
            run("pe", e)

        @block.scalar
        def _(e):
            run("act", e)

        @block.vector
        def _(e):
            run("dve", e)

        @block.gpsimd
        def _(e):
            run("pool", e)

        @block.sync
        def _(e):
            run("sp", e)


def build_program(stop=99):
    nc = bass.Bass("TRN2", target_bir_lowering=False)
    pr = Prog()

    def dram(name, shape, dt=F32, kind="ExternalInput"):
        return nc.dram_tensor(name, shape, dt, kind=kind).ap()

    xT = dram("xT", [D, S])
    pT = dram("pT", [256, S])
    w_in = dram("w_in", [D, 2312])
    w_out = dram("w_out", [D, D])
    w_ff1 = dram("w_ff1", [D, 4096])
    w_ff2 = dram("w_ff2", [4096, D])
    w_ple = dram("w_ple", [256, D])
    w_gate = dram("w_gate", [D, D])
    gT = dram("gT", [128, 40])
    bfg = dram("bfg", [8, 1])
    sinks = dram("sinks", [128, 8])
    biasT = dram("biasT", [128, 2048])
    maskF = dram("maskF", [128, 128])
    identD = dram("ident", [128, 128])
    yT = dram("yT", [D, S], kind="ExternalOutput")
    augD = dram("augscr", [8, 3, S], BF16, kind="Internal")

    from contextlib import ExitStack

    stack = ExitStack()
    with stack:
        def sb(name, shape, dt):
            return stack.enter_context(nc.sbuf_tensor(name, shape, dt))

        hT = sb("hT", [128, 8, S], F32)
        X1 = sb("X1", [128, 8192], BF16)
        X2 = sb("X2", [128, 8192], BF16)
        ZO = sb("ZO", [128, 32768], BF16)
        NWS = 4
        WS = [sb("WS%d" % i, [128, 4096], BF16) for i in range(NWS)]
        G = sb("G", [128, 40], F32)
        IDB = sb("IDB", [128, 128], BF16)
        MFB = sb("MFB", [128, 128], BF16)
        ONESB = sb("ONESB", [128, 128], BF16)
        SWAP = sb("SWAP", [128, 128], BF16)
        SKE = sb("SKE", [128, 4], F32)
        SK = sb("SK", [128, 8], F32)
        BFt = sb("BFt", [8, 1], F32)
        NBt = sb("NBt", [8, 1], F32)
        NRS = 1
        NSQ = 2
        NPT = 2
        RS = [sb("RS%d" % i, [128, 512], F32) for i in range(NRS)]
        SQ = [sb("SQ%d" % i, [128, 512], BF16) for i in range(NSQ)]
        PTP = [sb("PT%d" % i, [128, 2, 512], BF16) for i in range(NPT)]
        RE = sb("RE", [128, 512], BF16)
        RO = sb("RO", [128, 512], BF16)
        RT = [RE, RO]
        DEN = [RE, RO]
        DT = sb("DT", [128, 512], F32)
        BCSt = sb("BCS", [128, 1024], BF16)
        RC = [DT, BCSt.bitcast(F32)]
        GT = RC
        PSa = stack.enter_context(nc.psum_tensor("psa", [128, 1024], F32))
        PSb = stack.enter_context(nc.psum_tensor("psb", [128, 1024], F32))
        PS = [PSa[:, 0:512], PSa[:, 512:1024], PSb[:, 0:512], PSb[:, 512:1024]]
        PS += [stack.enter_context(nc.psum_tensor("ps%d" % i, [128, 512], F32))[:, :] for i in range(4, 8)]
        PSP = [PSa[:, :].rearrange("p (a x) -> p a x", a=2), PSb[:, :].rearrange("p (a x) -> p a x", a=2)]
        ACCA, ACCB = 4, 5

        X2f = X2.bitcast(F32)
        ZOf = ZO.bitcast(F32)
        MIX = X2f[:, 0:4096].rearrange("p (m t) -> p m t", m=8)

        def aT(k, c0, c1):
            t = X1 if k < 4 else X2
            kk = k % 4
            return t[:, kk * 2048 + c0: kk * 2048 + c1]

        OT = ZO[:, 0:16384].rearrange("p (c t) -> p c t", c=8)
        ZB = 16384
        BM = ZO[:, ZB + 13312: ZB + 15360]
        hTb = hT.bitcast(BF16)[:, :, :].rearrange("p k t -> p (k t)")
        UB = [ZO[:, ZB: ZB + 13312], hTb[:, 0:13312]]

        def swa_views(b):
            B_ = UB[b]
            return (B_[:, 0:4096].rearrange("p (c t) -> p c t", c=2),
                    B_[:, 4096:8192].rearrange("p (c t) -> p c t", c=2),
                    B_[:, 8192:10240],
                    B_[:, 10240:13312].rearrange("p (t c) -> p t c", t=16))

        def fox_views(b):
            B_ = UB[b]
            return (B_[:, 0:2048], B_[:, 2048:4096], B_[:, 4096:6144], B_[:, 6144:8192],
                    B_[:, 8192:11264].rearrange("p (t c) -> p t c", t=16))
        T1 = ZOf[0:8, 0:2048]
        T2 = ZOf[0:8, 2048:4096]
        A3 = ZO[0:8, 8192: 8192 + 6144].rearrange("p (r t) -> p r t", r=3)
        mT = X1[:, 0:8192].rearrange("p (k t) -> p k t", k=8)
        U = ZO[:, 0:32768].rearrange("p (f t) -> p f t", f=32)
        HB = X1[:, 0:4096].rearrange("p (k t) -> p k t", k=8)
        MIXB = ZOf[:, 0:4096].rearrange("p (m t) -> p m t", m=8)
        HB2 = ZO[:, 8192:12288].rearrange("p (k t) -> p k t", k=8)

        def phase(n):
            pr.enabled = n <= stop

        def nfree(ap):
            n = 1
            for d in list(ap.shape)[1:]:
                n *= int(d)
            return n

        def mm(out, lhsT, rhs, start, stop, reads, writes):
            c = max(nfree(rhs), 64) / 2.4 + 8.0
            return pr.add("pe", lambda e: e.matmul(out, lhsT, rhs, start=start, stop=stop,
                                                   skip_group_check=True), reads, writes, cost=c)

        def act_fn(out, in_, func, bias=None, scale=None):
            kw = {}
            if bias is not None:
                kw["bias"] = bias
            if scale is not None:
                kw["scale"] = scale
            return lambda e: e.activation(out, in_, func, **kw)

        def act_cost(out):
            return 210.0 + 0.833 * nfree(out)

        def act(out, in_, func, reads, writes, bias=None, scale=None):
            return pr.add("act", act_fn(out, in_, func, bias, scale), reads, writes, cost=act_cost(out))

        def dve_fn(fnname, *args, **kw):
            return lambda e: getattr(e, fnname)(*args, **kw)

        def dve_cost(out, fast=False):
            return 90.0 + nfree(out) * (0.52 if fast else 1.04)

        def dve(fnname, reads, writes, *args, **kw):
            fast = kw.pop("fast", False)
            return pr.add("dve", dve_fn(fnname, *args, **kw), reads, writes, cost=dve_cost(args[0], fast))

        cur_pen = [0.0]

        def anyop(reads, writes, out, act_args, dve_name, dve_args, fast=False):
            i = pr.add("act", act_fn(*act_args), reads, writes, cost=act_cost(out),
                       alt=("dve", dve_fn(dve_name, *dve_args), dve_cost(out, fast)))
            if i is not None:
                pr.ops[i]["pen"] = cur_pen[0]
            return i

        U32 = mybir.dt.uint32
        BITS = {1.0: 0x3F803F80, -1.0: 0xBF80BF80}

        def zero_fill(ap, reads, writes):
            n2 = nfree(ap) // 2
            i = pr.add("act", (lambda e: e.memzero(ap)), reads, writes, cost=160.0 + 0.833 * n2,
                       alt=("dve", (lambda e: e.memzero(ap)), 140.0 + 1.04 * n2))
            if i is not None:
                pr.ops[i]["pen"] = 0.0
            return i

        def const_fill(ap, val, reads, writes):
            apu = ap.bitcast(U32)
            return pr.add("dve", (lambda e: e.memset(apu, BITS[val])), reads, writes,
                          cost=140.0 + 1.04 * (nfree(ap) // 2))

        wcount = [0]

        def wload(dmas):
            s = wcount[0] % NWS
            wcount[0] += 1
            t = WS[s]

            def fn(e):
                r = []
                for ent in dmas:
                    if len(ent) == 2:
                        dst = ent[0](t)
                        src = ent[1]
                    else:
                        (c0, K, C, src) = ent
                        dst = t[:, c0: c0 + K * C].rearrange("p (k c) -> p k c", k=K)
                    r.append(e.dma_start(out=dst, in_=src))
                return r
            nb = 0
            for ent in dmas:
                nb += nfree(ent[1]) * 128 * 4 if len(ent) == 2 else ent[1] * ent[2] * 128 * 4
            rd = [("h", 0, 3)] if wcount[0] in (3, 4) else []
            i = pr.add("pool", fn, reads=rd, writes=[("ws", s)], dma="w%d" % s, nobarrier=True,
                       cost=1200.0 * len(dmas), lat=2500.0 + nb / 300.0)
            if i is not None:
                pr.ops[i]["nbytes"] = nb
            if i is not None:
                pr.ops[i]["ndma"] = len(dmas)
            return t, ("ws", s)

        def spdma(out, in_, reads, writes, group):
            i = pr.add("sp", lambda e: e.dma_start(out=out, in_=in_), reads, writes, dma=group,
                       cost=150.0, lat=2200.0 + nfree(out) * 128 * 4 / 200.0)
            if i is not None:
                pr.ops[i]["ndma"] = 1
                pr.ops[i]["nbytes"] = nfree(out) * int(out.shape[0]) * 4
            return i

        def wview(t, c0, K, C):
            return t[:, c0: c0 + K * C].rearrange("p (k c) -> p k c", k=K)

        def wsrc(w, r0, nrows, c0, c1):
            return w[r0: r0 + nrows, c0:c1].rearrange("(k p) c -> p k c", p=128)

        gen_i = [0]
        GEN = [6, 7]

        def genbank():
            b = GEN[gen_i[0] % len(GEN)]
            gen_i[0] += 1
            return b

        sq_i = [0]

        def sqtile():
            i = sq_i[0] % NSQ
            sq_i[0] += 1
            return i

        rs_i = [0]

        SQPOOL = [(SQ[i][:, :], ("sq", i)) for i in range(NSQ)]
        sqp_i = [0]

        def stats_add(src, src_reads, msb, first, last, sbsrc=False, n=512):
            sqt, sqk = SQPOOL[sqp_i[0] % len(SQPOOL)]
            sqt = sqt[:, 0:n]
            sqp_i[0] += 1
            if sbsrc:
                anyop(src_reads, [sqk], sqt, (sqt, src, AF.Square), "tensor_tensor", (sqt, src, src, ALU.mult))
            else:
                act(sqt, src, AF.Square, src_reads, [sqk])
            mm(PS[msb][:, 0:n], ONESB[:, :], sqt, first, last, [sqk, ("c", "ones")], [("ps", msb)])

        def rstd_from(msb, alt=False, eps=EPS, n=512):
            if alt:
                t, key = DT, ("rs2",)
            else:
                i = rs_i[0] % NRS
                rs_i[0] += 1
                t, key = RS[i], ("rs", i)
            dve("tensor_scalar", [("ps", msb)], [key], t[:, 0:n], PS[msb][:, 0:n], 1.0 / D, eps,
                ALU.mult, ALU.add)
            act(t[:, 0:n], t[:, 0:n], AF.Ln, [key], [key])
            act(t[:, 0:n], t[:, 0:n], AF.Exp, [key], [key], scale=-0.5)
            return t, key

        def prenorm(gidx, tg, dst_fn, dst_keys, alt=False):
            msb = genbank()
            for k in range(8):
                stats_add(hT[:, k, tg * 512:(tg + 1) * 512], [("h", k, tg), ("pno", k)], msb, k == 0, k == 7,
                          sbsrc=True)
            rt_, rk_ = rstd_from(msb, alt)
            for k in range(8):
                dve("scalar_tensor_tensor", [("h", k, tg), rk_, ("c", "g")], [dst_keys(k), ("pno", k)],
                    dst_fn(k), hT[:, k, tg * 512:(tg + 1) * 512], G[:, gidx * 8 + k: gidx * 8 + k + 1],
                    rt_[:, :], ALU.mult, ALU.mult)

        def postnorm_update(gidx, tg, msb, mixt=None, mkey="mix", after=None, eps=EPS, xreads=None,
                            t0=None, n=512, wkey=None):
            if mixt is None:
                mixt = MIX
            if t0 is None:
                t0 = tg * 512
            rt_, rk_ = rstd_from(msb, eps=eps, n=n)
            for mc in range(8):
                dve("scalar_tensor_tensor", [(mkey, mc), rk_, ("c", "g")] + (xreads(mc) if xreads else []),
                    [(mkey, mc)],
                    mixt[:, mc, 0:n], mixt[:, mc, 0:n], G[:, gidx * 8 + mc: gidx * 8 + mc + 1], rt_[:, 0:n],
                    ALU.mult, ALU.mult)
                wk = [("h", mc, tg)] if wkey is None else [wkey(mc)]
                dve("tensor_tensor", [(mkey, mc), ("h", mc, tg)], wk,
                    hT[:, mc, t0:t0 + n], hT[:, mc, t0:t0 + n], mixt[:, mc, 0:n], ALU.add)
                if after is not None:
                    after(mc)

        phase(0)
        xv = xT[:, :].rearrange("(k p) t -> p k t", p=128)
        def xload(t):
            o_ = hT[:, :, t * 512:(t + 1) * 512]
            i_ = xv[:, :, t * 512:(t + 1) * 512]
            i = pr.add("pool", (lambda e: e.dma_start(out=o_, in_=i_)), [], [("h", k, t) for k in range(8)],
                       dma="x%d" % t, cost=1200.0, lat=2500.0)
            if i is not None:
                pr.ops[i]["ndma"] = 1
                pr.ops[i]["nbytes"] = 2 * 1024 * 1024
        spdma(G[:, :], gT[:, :], [], [("c", "g")], "c0")
        xload(0)
        spdma(BFt[:, :], bfg[:, :], [], [("c", "bf")], "c1")
        spdma(SK[:, :], sinks[:, :], [], [("c", "sk")], "c2")
        xload(1)
        xload(2)
        xload(3)
        spdma(ZOf[:, 0:128], identD[:, :], [], [("tmp", "id")], "c3")
        spdma(ZOf[:, 128:256], maskF[:, :], [], [("tmp", "mf")], "c3")
        i_bm = pr.add("pool", (lambda e: e.dma_start(out=BM, in_=biasT[:, :])), [], [("c", "bm")], dma="bm",
                      cost=1200.0, lat=2500.0)
        if i_bm is not None:
            pr.ops[i_bm]["ndma"] = 1
            pr.ops[i_bm]["nbytes"] = 1024 * 1024

        dve("memset", [], [("c", "ones")], ONESB[:, :], 1.0)

        phase(1)
        SQPOOL.extend([(RE[:, :], ("den", 0, "s", 0)), (RO[:, :], ("den", 1, "s", 0)),
                       (BCSt[:, 0:512], ("rc", "s", 0)), (BCSt[:, 512:1024], ("rc", "s", 1))])
        for tg in range(4):
            prenorm(0, tg, lambda k, tg=tg: aT(k, tg * 512, (tg + 1) * 512), lambda k, tg=tg: ("a", k, tg))
        del SQPOOL[NSQ:]
        pr.retire("DENR")
        pr.retire("RCR")
        LATE = [("a", 7, 3)]
        dve("tensor_copy", [("tmp", "id")] + LATE, [("c", "id")], IDB[:, :], ZOf[:, 0:128])
        dve("tensor_copy", [("tmp", "mf")] + LATE, [("c", "mf")], MFB[:, :], ZOf[:, 128:256])
        dve("tensor_copy", [("c", "id")], [("c", "swap")], SWAP[:, 0:64], IDB[:, 64:128])
        dve("tensor_copy", [("c", "id"), ("c", "swap")], [("c", "swap")], SWAP[:, 64:128], IDB[:, 0:64])
        dve("tensor_scalar", [("c", "bf")] + LATE, [("c", "nb")], NBt[:, :], BFt[:, :], -1.0, None, ALU.mult)
        act(SK[:, :], SK[:, :], AF.Exp, [("c", "sk")] + LATE, [("c", "sk")])
        for g in range(2):
            for c in range(2):
                he = 4 * g + 2 * c
                dve("tensor_copy", [("c", "sk"), ("c", "ske")], [("c", "ske")], SKE[64:128, g * 2 + c: g * 2 + c + 1],
                    SK[64:128, he:he + 1])
                dve("tensor_copy", [("c", "sk"), ("c", "ske")], [("c", "ske")], SKE[0:64, g * 2 + c: g * 2 + c + 1],
                    SK[0:64, he + 1:he + 2])
        pr.retire("ZO_O")

        a_all = [("a", k, t) for k in range(8) for t in range(4)]

        phase(2)
        wt, wkey = wload([(0, 8, 128, wsrc(w_in, 0, D, 1536, 1664))])
        wv = wview(wt, 0, 8, 128)
        for tg in range(4):
            b = genbank()
            for k in range(8):
                mm(PS[b][:, :], wv[:, k, :], aT(k, tg * 512, (tg + 1) * 512), k == 0, k == 7,
                   [wkey, ("a", k, tg)], [("ps", b)])
            act(T1[:, tg * 512:(tg + 1) * 512], PS[b][0:8, :], AF.Exp, [("ps", b), ("c", "nb")], [("t1",)],
                bias=NBt[:, 0:1], scale=-1.0)
        act(T1, T1, AF.Ln, [("t1",)], [("t1",)], bias=1.0, scale=1.0)
        dve("tensor_tensor_scan", [("t1",)], [("t2",)], T2, T1, T1, 0.0, ALU.add, ALU.max)
        dve("tensor_copy", [("t2",)], [("a3", 0)], A3[:, 0, :], T2)
        dve("tensor_tensor", [("t2",), ("a3", 0)], [("t1",)], T1, T2, A3[:, 0, :], ALU.subtract)
        dve("tensor_copy", [("t1",)], [("a3", 1)], A3[:, 1, :], T1)
        dve("tensor_tensor", [("t1",), ("a3", 1)], [("t2",)], T2, T1, A3[:, 1, :], ALU.subtract)
        dve("tensor_copy", [("t2",)], [("a3", 2)], A3[:, 2, :], T2)
        spdma(augD[:, :, :], A3, [("a3", 0), ("a3", 1), ("a3", 2)], [("augD",)], "aug")
        pr.retire("ZO_O")

        phase(3)
        cur_pen[0] = 300.0
        sbanks = [0, 1, 2]
        s_i = [0]
        sp_i = [0]
        pt_i = [0]
        ev_i = [0]

        def evac_scaled(out, in_, reads, writes, scale):
            anyop(reads, writes, out, (out, in_, AF.Copy, None, scale), "tensor_scalar",
                  (out, in_, scale, None, ALU.mult))

        for g in range(2):
            phase(3)
            pr.cur_buf = [0, 1][g]
            if g == 1:
                pr.retire("B1")
            Qe, Qo, KD, VS = swa_views(pr.cur_buf)
            zero_fill(Qe[64:128, :, :], [], [("Qe", "pad")])
            zero_fill(Qo[0:64, :, :], [], [("Qo", "pad")])
            const_fill(VS[:, :, 64:128], 1.0, [], [("VS", "ones")])
            qc0 = 1544 + 256 * g
            kc0 = 2056 + 64 * g
            vc0 = 2184 + 64 * g
            wt, wkey = wload([(0, 8, 256, wsrc(w_in, 0, D, qc0, qc0 + 256)),
                              (lambda t: t[:, 2048:3072].rearrange("p (k c) -> p k c", k=8)[:, :, 0:64],
                               wsrc(w_in, 0, D, kc0, kc0 + 64)),
                              (lambda t: t[:, 2048:3072].rearrange("p (k c) -> p k c", k=8)[:, :, 64:128],
                               wsrc(w_in, 0, D, kc0, kc0 + 64)),
                              (3072, 8, 64, wsrc(w_in, 0, D, vc0, vc0 + 64))])
            wq = wview(wt, 0, 8, 256)
            wkk = wview(wt, 2048, 8, 128)
            wvv = wview(wt, 3072, 8, 64)
            phase(3.02)
            for c in range(2):
                for tg in range(4):
                    b = genbank()
                    for k in range(8):
                        mm(PS[b][:, :], wq[:, k, c * 128:(c + 1) * 128], aT(k, tg * 512, (tg + 1) * 512),
                           k == 0, k == 7, [wkey, ("a", k, tg)], [("ps", b)])
                    evac_scaled(Qe[0:64, c, tg * 512:(tg + 1) * 512], PS[b][0:64, :], [("ps", b)],
                                [("Qe", c, tg)], 0.125)
                    evac_scaled(Qo[64:128, c, tg * 512:(tg + 1) * 512], PS[b][64:128, :], [("ps", b)],
                                [("Qo", c, tg)], 0.125)
            phase(3.03)
            for tg in range(4):
                b2 = genbank()
                for k in range(8):
                    mm(PS[b2][:, :], wkk[:, k, :],
                       aT(k, tg * 512, (tg + 1) * 512), k == 0, k == 7, [wkey, ("a", k, tg)], [("ps", b2)])
                evac_scaled(KD[:, tg * 512:(tg + 1) * 512], PS[b2][:, :], [("ps", b2)], [("KD", tg)], 1.0)
            phase(3.04)
            for ttg in range(2):
                b = genbank()
                pv = PS[b][:, :].rearrange("p (i c) -> p i c", i=8)
                for i in range(8):
                    tt = ttg * 8 + i
                    for k in range(8):
                        mm(pv[:, i, :], aT(k, tt * 128, (tt + 1) * 128), wvv[:, k, :], k == 0, k == 7,
                           [wkey, ("a", k, tt // 4)], [("ps", b)])
                evac_scaled(VS[:, ttg * 8:(ttg + 1) * 8, 0:64], pv, [("ps", b)], [("VS", ttg, 0)], 1.0)
                evac_scaled(VS[:, ttg * 8:(ttg + 1) * 8, 128:192], pv, [("ps", b)], [("VS", ttg, 1)], 1.0)
            BMv = BM.rearrange("p (t g x) -> p t g x", t=2, g=2)
            ch = 4 + 2 * g
            for n0 in range(0, 16, 2):
                phase(3.2)
                for jn in range(2):
                    n = n0 + jn
                    tiles = ([0] if n > 0 else []) + [1]
                    pi = sp_i[0] % 2
                    sp_i[0] += 1
                    pti = pt_i[0] % NPT
                    pt_i[0] += 1
                    pts = []
                    for ti, tl in enumerate(tiles):
                        kb = n - 1 if tl == 0 else n
                        sbk = 2 * pi + ti
                        mm(PS[sbk][:, :], IDB[:, :], BMv[:, tl, g, :], True, False, [("c", "id"), ("c", "bm")],
                           [("ps", sbk)])
                        mm(PS[sbk][:, 0:256].rearrange("p (c t) -> p c t", c=2), KD[:, kb * 128:(kb + 1) * 128],
                           Qe[:, :, n * 128:(n + 1) * 128], False, False,
                           [("KD", kb // 4), ("Qe", 0, n // 4), ("Qe", 1, n // 4), ("Qe", "pad")], [("ps", sbk)])
                        mm(PS[sbk][:, 256:512].rearrange("p (c t) -> p c t", c=2), KD[:, kb * 128:(kb + 1) * 128],
                           Qo[:, :, n * 128:(n + 1) * 128], False, True,
                           [("KD", kb // 4), ("Qo", 0, n // 4), ("Qo", 1, n // 4), ("Qo", "pad")], [("ps", sbk)])
                        pts.append((ti, kb))
                    nt = len(tiles)
                    act(PTP[pti][:, 0:nt, :], PSP[pi][:, 0:nt, :], AF.Exp,
                        [("ps", 2 * pi + ti) for ti in range(nt)], [("pt", pti)])
                    for ii, (ti, kb) in enumerate(pts):
                        mm(PS[ACCA][:, jn * 256:(jn + 1) * 256], VS[:, kb, 0:128], PTP[pti][:, ti, 0:256],
                           ii == 0, ii == len(pts) - 1,
                           [("pt", pti), ("VS", kb // 8, 0), ("VS", "ones")], [("ps", ACCA)])
                        mm(PS[ACCB][:, jn * 256:(jn + 1) * 256], VS[:, kb, 64:192], PTP[pti][:, ti, 256:512],
                           ii == 0, ii == len(pts) - 1,
                           [("pt", pti), ("VS", kb // 8, 1), ("VS", "ones")], [("ps", ACCB)])
                phase(3.3)
                di = (n0 // 2) % 2
                for c in range(2):
                    dve("tensor_scalar", [("ps", ACCA), ("c", "ske")], [("den", di, 1, c)],
                        DEN[di][64:128, :].rearrange("p (j c t) -> p j c t", j=2, c=2)[:, :, c, :],
                        PS[ACCA][64:128, :].rearrange("p (j c t) -> p j c t", j=2, c=2)[:, :, c, :],
                        SKE[64:128, g * 2 + c: g * 2 + c + 1], None, ALU.add)
                    dve("tensor_scalar", [("ps", ACCB), ("c", "ske")], [("den", di, 0, c)],
                        DEN[di][0:64, :].rearrange("p (j c t) -> p j c t", j=2, c=2)[:, :, c, :],
                        PS[ACCB][0:64, :].rearrange("p (j c t) -> p j c t", j=2, c=2)[:, :, c, :],
                        SKE[0:64, g * 2 + c: g * 2 + c + 1], None, ALU.add)
                swb = 2 * (sp_i[0] % 2)
                sp_i[0] += 1
                mm(PS[swb][:, :], SWAP[:, :], DEN[di][:, :], True, True,
                   [("den", di, a, c) for a in range(2) for c in range(2)] + [("c", "swap")], [("ps", swb)])
                act(RC[di][:, :], PS[swb][:, :], AF.Ln, [("ps", swb)], [("rc", di)])
                act(RC[di][:, :], RC[di][:, :], AF.Exp, [("rc", di)], [("rc", di)], scale=-1.0)
                tgk = n0 // 4
                dve("tensor_tensor", [("ps", ACCA), ("rc", di)], [("OT", ch, tgk, 0), ("OT", ch + 1, tgk, 0)],
                    OT[0:64, ch:ch + 2, n0 * 128:(n0 + 2) * 128].rearrange("p c (j t) -> p j c t", j=2),
                    PS[ACCA][0:64, :].rearrange("p (j c t) -> p j c t", j=2, c=2),
                    RC[di][0:64, :].rearrange("p (j c t) -> p j c t", j=2, c=2), ALU.mult)
                dve("tensor_tensor", [("ps", ACCB), ("rc", di)], [("OT", ch, tgk, 1), ("OT", ch + 1, tgk, 1)],
                    OT[64:128, ch:ch + 2, n0 * 128:(n0 + 2) * 128].rearrange("p c (j t) -> p j c t", j=2),
                    PS[ACCB][64:128, :].rearrange("p (j c t) -> p j c t", j=2, c=2),
                    RC[di][64:128, :].rearrange("p (j c t) -> p j c t", j=2, c=2), ALU.mult)

        phase(4)
        for p in range(4):
            hA, hB = 2 * p, 2 * p + 1
            pr.cur_buf = [0, 1, 0, 1][p]
            qE, qO, kE, kO, VF = fox_views(pr.cur_buf)
            if p < 2:
                pr.retire("B%d" % pr.cur_buf)
                const_fill(qE[64:96, :], 1.0, [], [("qE", "aug")])
                zero_fill(kE[64:96, :], [], [("kE", "aug")])
                const_fill(kE[64:67, :], -1.0, [("kE", "aug")], [("kE", "aug")])
                zero_fill(qO[0:64, :], [], [("qO", "aug")])
                const_fill(qO[0:32, :], 1.0, [("qO", "aug")], [("qO", "aug")])
                zero_fill(kO[0:64, :], [], [("kO", "aug")])
                const_fill(kO[0:3, :], -1.0, [("kO", "aug")], [("kO", "aug")])
                const_fill(VF[:, :, 64:128], 1.0, [], [("VF", "ones")])
            wt, wkey = wload([(0, 8, 128, wsrc(w_in, 0, D, 128 * p, 128 * p + 128)),
                              (1024, 8, 128, wsrc(w_in, 0, D, 512 + 128 * p, 512 + 128 * p + 128)),
                              (2048, 8, 128, wsrc(w_in, 0, D, 1024 + 128 * p, 1024 + 128 * p + 128))])
            wq = wview(wt, 0, 8, 128)
            wk = wview(wt, 1024, 8, 128)
            wvv = wview(wt, 2048, 8, 128)
            spdma(qE[64:67, :], augD[hA, :, :], [("augD",), ("qE", "aug")], [("qE", "aug")], "aug")
            spdma(qO[0:3, :], augD[hB, :, :], [("augD",), ("qO", "aug")], [("qO", "aug")], "aug")
            spdma(kE[67:70, :], augD[hA, :, :], [("augD",), ("kE", "aug")], [("kE", "aug")], "aug")
            spdma(kO[3:6, :], augD[hB, :, :], [("augD",), ("kO", "aug")], [("kO", "aug")], "aug")
            for tg in range(4):
                b = genbank()
                for k in range(8):
                    mm(PS[b][:, :], wq[:, k, :], aT(k, tg * 512, (tg + 1) * 512), k == 0, k == 7,
                       [wkey, ("a", k, tg)], [("ps", b)])
                evac_scaled(qE[0:64, tg * 512:(tg + 1) * 512], PS[b][0:64, :], [("ps", b)], [("qE", tg)], 0.125)
                evac_scaled(qO[64:128, tg * 512:(tg + 1) * 512], PS[b][64:128, :], [("ps", b)], [("qO", tg)], 0.125)
                b = genbank()
                for k in range(8):
                    mm(PS[b][:, :], wk[:, k, :], aT(k, tg * 512, (tg + 1) * 512), k == 0, k == 7,
                       [wkey, ("a", k, tg)], [("ps", b)])
                evac_scaled(kE[0:64, tg * 512:(tg + 1) * 512], PS[b][0:64, :], [("ps", b)], [("kE", tg)], 1.0)
                evac_scaled(kO[64:128, tg * 512:(tg + 1) * 512], PS[b][64:128, :], [("ps", b)], [("kO", tg)], 1.0)
            for ttg in range(4):
                b = genbank()
                pv = PS[b][:, :].rearrange("p (i c) -> p i c", i=4)
                for i in range(4):
                    tt = ttg * 4 + i
                    for k in range(8):
                        mm(pv[:, i, :], aT(k, tt * 128, (tt + 1) * 128), wvv[:, k, :], k == 0, k == 7,
                           [wkey, ("a", k, tt // 4)], [("ps", b)])
                evac_scaled(VF[:, ttg * 4:(ttg + 1) * 4, 0:64], pv[:, :, 0:64], [("ps", b)], [("VF", ttg, 0)], 1.0)
                evac_scaled(VF[:, ttg * 4:(ttg + 1) * 4, 128:192], pv[:, :, 64:128], [("ps", b)], [("VF", ttg, 1)], 1.0)

            for tg in range(4):
                nj = 4 * tg + 4
                def emit_PV(st, nj=nj):
                    j, pti, c0, N = st
                    mm(PS[ACCA][:, c0:512], VF[:, j, 0:128], PTP[pti][:, 0, 0:N], j == 0, j == nj - 1,
                       [("pt", pti), ("VF", j // 4, 0), ("VF", "ones")], [("ps", ACCA)])
                    mm(PS[ACCB][:, c0:512], VF[:, j, 64:192], PTP[pti][:, 1, 0:N], j == 0, j == nj - 1,
                       [("pt", pti), ("VF", j // 4, 1), ("VF", "ones")], [("ps", ACCB)])

                pend = []
                for j in range(nj):
                    r = j - 4 * tg
                    c0 = 128 * r if r >= 0 else 0
                    N = 512 - c0
                    pi = sp_i[0] % 2
                    sp_i[0] += 1
                    pti = pt_i[0] % NPT
                    pt_i[0] += 1
                    for hd in (0, 1):
                        sbk = 2 * pi + hd
                        if hd == 0:
                            qt, kt, K = qE, kE, 96
                            rk = [("qE", tg), ("qE", "aug"), ("kE", j // 4), ("kE", "aug")]
                        else:
                            qt, kt, K = qO, kO, 128
                            rk = [("qO", tg), ("qO", "aug"), ("kO", j // 4), ("kO", "aug")]
                        mm(PS[sbk][:, 0:N], kt[0:K, j * 128:(j + 1) * 128], qt[0:K, tg * 512 + c0:(tg + 1) * 512],
                           True, r < 0, rk, [("ps", sbk)])
                        if r >= 0:
                            mm(PS[sbk][:, 0:128], IDB[:, :], MFB[:, :], False, True, [("c", "id"), ("c", "mf")],
                               [("ps", sbk)])
                    act(PTP[pti][:, :, 0:N], PSP[pi][:, :, 0:N], AF.Exp, [("ps", 2 * pi), ("ps", 2 * pi + 1)],
                        [("pt", pti)])
                    pend.append((j, pti, c0, N))
                    if len(pend) > 1:
                        emit_PV(pend.pop(0))
                while pend:
                    emit_PV(pend.pop(0))
                di = (p * 4 + tg) % 2
                anyop([("ps", ACCA)], [("den", di, 1, 0), ("den", di, 1, 1)], DEN[di][64:128, :],
                      (DEN[di][64:128, :], PS[ACCA][64:128, :], AF.Copy), "tensor_copy",
                      (DEN[di][64:128, :], PS[ACCA][64:128, :]))
                anyop([("ps", ACCB)], [("den", di, 0, 0), ("den", di, 0, 1)], DEN[di][0:64, :],
                      (DEN[di][0:64, :], PS[ACCB][0:64, :], AF.Copy), "tensor_copy",
                      (DEN[di][0:64, :], PS[ACCB][0:64, :]))
                swb = 2 * (sp_i[0] % 2)
                sp_i[0] += 1
                mm(PS[swb][:, :], SWAP[:, :], DEN[di][:, :], True, True,
                   [("den", di, a, c) for a in range(2) for c in range(2)] + [("c", "swap")], [("ps", swb)])
                act(RC[di][:, :], PS[swb][:, :], AF.Ln, [("ps", swb)], [("rc", di)])
                act(RC[di][:, :], RC[di][:, :], AF.Exp, [("rc", di)], [("rc", di)], scale=-1.0)
                dve("tensor_tensor", [("ps", ACCA), ("rc", di)], [("OT", p, tg, 0)],
                    OT[0:64, p, tg * 512:(tg + 1) * 512], PS[ACCA][0:64, :], RC[di][0:64, :], ALU.mult)
                dve("tensor_tensor", [("ps", ACCB), ("rc", di)], [("OT", p, tg, 1)],
                    OT[64:128, p, tg * 512:(tg + 1) * 512], PS[ACCB][64:128, :], RC[di][64:128, :], ALU.mult)
        phase(4.9)
        pr.retire("B1")
        for k in range(4):
            spdma(hT[:, k, :], xT[k * 128:(k + 1) * 128, :], [], [("h", k, t) for t in range(4)], "xr%d" % k)
        pr.retire("X2")

        phase(5)
        GEN[:] = [6, 0, 1, 2, 3, 4, 5]
        wo_p = []
        for hh in range(2):
            wt, wkey = wload([(0, 8, 512, wsrc(w_out, 0, D, hh * 512, (hh + 1) * 512))])
            wo_p.append((wview(wt, 0, 8, 512), wkey))
        for tg in range(4):
            msb = 7
            for mc in range(8):
                b = genbank()
                for c in range(8):
                    wo, wkey = wo_p[mc // 4]
                    rk = [wkey, ("OT", c, tg, 0), ("OT", c, tg, 1)]
                    mm(PS[b][:, :], wo[:, c, (mc % 4) * 128:(mc % 4 + 1) * 128], OT[:, c, tg * 512:(tg + 1) * 512],
                       c == 0, c == 7, rk, [("ps", b)])
                anyop([("ps", b)], [("mix", mc)], MIX[:, mc, :], (MIX[:, mc, :], PS[b][:, :], AF.Copy), "tensor_copy",
                      (MIX[:, mc, :], PS[b][:, :]))
                stats_add(PS[b][:, :], [("ps", b)], msb, mc == 0, mc == 7)
            postnorm_update(1, tg, msb)
        pr.retire("X1")
        pr.retire("ZO_O")
        pr.retire("B0")
        pr.retire("BMR")
        pr.retire("DENR")
        pr.retire("RCR")

        phase(6)
        GEN[:] = [0, 1, 2, 3, 4, 6, 7]
        cur_pen[0] = 0.0
        for hf in range(2):
            for stg in range(2):
                tg = 2 * hf + stg
                prenorm(2, tg, lambda k, stg=stg: mT[:, k, stg * 512:(stg + 1) * 512],
                        lambda k, stg=stg: ("m", k, stg), alt=True)
            for wp in range(8):
                wt, wkey = wload([(0, 8, 512, wsrc(w_ff1, 0, D, wp * 512, (wp + 1) * 512))])
                w1 = wview(wt, 0, 8, 512)
                for fcl in range(4):
                    fc = wp * 4 + fcl
                    for stg in range(2):
                        b = genbank()
                        for k in range(8):
                            mm(PS[b][:, :], w1[:, k, fcl * 128:(fcl + 1) * 128], mT[:, k, stg * 512:(stg + 1) * 512],
                               k == 0, k == 7, [wkey, ("m", k, stg)], [("ps", b)])
                        ri = (fc * 2 + stg) % 2
                        anyop([("ps", b)], [("rt", ri)], RT[ri][:, :], (RT[ri][:, :], PS[b][:, :], AF.Relu),
                              "tensor_scalar", (RT[ri][:, :], PS[b][:, :], 0.0, None, ALU.max))
                        anyop([("rt", ri)], [("u", fc, stg)], U[:, fc, stg * 512:(stg + 1) * 512],
                              (U[:, fc, stg * 512:(stg + 1) * 512], RT[ri][:, :], AF.Square), "tensor_tensor",
                              (U[:, fc, stg * 512:(stg + 1) * 512], RT[ri][:, :], RT[ri][:, :], ALU.mult), fast=True)
            for stg in range(2):
                tg = 2 * hf + stg
                msb = 5
                for wp in range(4):
                    bks = [genbank(), genbank()]
                    for fh in range(2):
                        wt, wkey = wload([(0, 16, 256, wsrc(w_ff2, fh * 2048, 2048, wp * 256, (wp + 1) * 256))])
                        w2 = wview(wt, 0, 16, 256)
                        for mcl in range(2):
                            b = bks[mcl]
                            for f16 in range(16):
                                f = fh * 16 + f16
                                mm(PS[b][:, :], w2[:, f16, mcl * 128:(mcl + 1) * 128],
                                   U[:, f, stg * 512:(stg + 1) * 512], f == 0, f == 31,
                                   [wkey, ("u", f, stg)], [("ps", b)])
                    for mcl in range(2):
                        mc = 2 * wp + mcl
                        b = bks[mcl]
                        anyop([("ps", b)], [("mix", mc)], MIX[:, mc, :], (MIX[:, mc, :], PS[b][:, :], AF.Copy),
                              "tensor_copy", (MIX[:, mc, :], PS[b][:, :]))
                        stats_add(PS[b][:, :], [("ps", b)], msb, mc == 0, mc == 7)
                postnorm_update(3, tg, msb)

        phase(7)
        pr.retire("X1")
        pr.retire("RCR")
        wg_p = []
        for hh in range(2):
            wtg, wgkey = wload([(0, 8, 512, wsrc(w_gate, 0, D, hh * 512, (hh + 1) * 512))])
            wg_p.append((wview(wtg, 0, 8, 512), wgkey))
        wtp, wpkey = wload([(0, 2, 1024, wsrc(w_ple, 0, 256, 0, D))])
        wpl = wview(wtp, 0, 2, 1024)
        wtp2, wpkey2 = wload([(0, 2, 2048, pT[:, :].rearrange("(k p) t -> p k t", p=128))])
        pTb = wview(wtp2, 0, 2, 2048)
        yv = yT[:, :].rearrange("(k p) t -> p k t", p=128)
        pr.retire("ZO_O")
        GEN[:] = [0, 1, 2, 3, 6, 7]
        ykeys = []

        GROUPS = [(0, 0, 512, None), (1, 512, 512, None), (2, 1024, 512, None), (3, 1536, 256, "a"), (3, 1792, 256, "b")]

        def ple_bufs(gix):
            par = gix % 2
            return ((MIX, "mix") if par == 0 else (MIXB, "mixb")) + ((HB, "hb") if par == 0 else (HB2, "hb2")) + \
                   ((5,) if par == 0 else (4,))

        def ple_front(gix):
            tg, t0, n, sub = GROUPS[gix]
            mixt, mkey, HBt, hkey, msb = ple_bufs(gix)
            for k in range(8):
                act(HBt[:, k, 0:n], hT[:, k, t0:t0 + n], AF.Copy, [("h", k, tg)], [(hkey, k)])
            for mc in range(8):
                b = genbank()
                for k in range(8):
                    wg, wgkey = wg_p[mc // 4]
                    mm(PS[b][:, 0:n], wg[:, k, (mc % 4) * 128:(mc % 4 + 1) * 128], HBt[:, k, 0:n], k == 0, k == 7,
                       [wgkey, (hkey, k)], [("ps", b)])
                gi = mc % 2
                act(GT[gi][:, 0:n], PS[b][:, 0:n], AF.Tanh, [("ps", b)], [("gt", gi)], scale=0.5)
                b2 = genbank()
                for j in range(2):
                    mm(PS[b2][:, 0:n], wpl[:, j, mc * 128:(mc + 1) * 128], pTb[:, j, t0:t0 + n],
                       j == 0, j == 1, [wpkey, wpkey2], [("ps", b2)])
                dve("scalar_tensor_tensor", [("ps", b2), ("gt", gi)], [(mkey, mc), ("plm", mc)], mixt[:, mc, 0:n],
                    GT[gi][:, 0:n], 1.0, PS[b2][:, 0:n], ALU.add, ALU.mult)
                stats_add(mixt[:, mc, 0:n], [(mkey, mc)], msb, mc == 0, mc == 7, sbsrc=False, n=n)

        def ple_post(gix):
            tg, t0, n, sub = GROUPS[gix]
            mixt, mkey, HBt, hkey, msb = ple_bufs(gix)
            xr = lambda mc: [("plm", mc)]
            en = pr.enabled
            if sub is None:
                postnorm_update(4, tg, msb, mixt, mkey, eps=4.0 * EPS, xreads=xr)
                pr.enabled = True
                o_ = yv[:, :, t0:t0 + n]
                i_ = hT[:, :, t0:t0 + n]
                i_o = pr.add("pool", (lambda e, o_=o_, i_=i_: e.dma_start(out=o_, in_=i_)),
                             [("h", k, tg) for k in range(8)], [("y", tg)], dma="out%d" % tg, cost=1200.0, lat=2500.0)
                if i_o is not None:
                    pr.ops[i_o]["ndma"] = 1
                    pr.ops[i_o]["nbytes"] = 2 * 1024 * 1024
                ykeys.append(("y", tg))
            else:
                def out_mc(mc, tg=tg, t0=t0, n=n, sub=sub):
                    o_ = yT[mc * 128:(mc + 1) * 128, t0:t0 + n]
                    i_ = hT[:, mc, t0:t0 + n]
                    if sub == "a":
                        i_o = pr.add("pool", (lambda e, o_=o_, i_=i_: e.dma_start(out=o_, in_=i_)),
                                     [("h3", sub, mc)], [("y", tg, sub, mc)], dma="o3%s_%d" % (sub, mc),
                                     cost=1200.0, lat=2500.0)
                        if i_o is not None:
                            pr.ops[i_o]["ndma"] = 1
                            pr.ops[i_o]["nbytes"] = 128 * n * 4
                    else:
                        spdma(o_, i_, [("h3", sub, mc)], [("y", tg, sub, mc)], "o3%s_%d" % (sub, mc))
                    ykeys.append(("y", tg, sub, mc))
                if pr.enabled:
                    postnorm_update(4, tg, msb, mixt, mkey, after=out_mc, eps=4.0 * EPS, xreads=xr, t0=t0, n=n,
                                    wkey=lambda mc, sub=sub: ("h3", sub, mc))
                elif sub == "a":
                    pr.enabled = True
                    for mc in range(8):
                        spdma(yT[mc * 128:(mc + 1) * 128, 1536:2048], hT[:, mc, 1536:2048],
                              [("h", mc, 3)], [("y", 3, "x", mc)], "o3a_%d" % mc)
                        ykeys.append(("y", 3, "x", mc))
            pr.enabled = en

        ple_front(0)
        ple_front(1)
        ple_post(0)
        ple_front(2)
        ple_post(1)
        ple_front(3)
        ple_post(2)
        ple_front(4)
        ple_post(3)
        ple_post(4)
        pr.enabled = True
        pr.add("sp", lambda e: None, reads=ykeys, writes=[])

        for op in pr.ops:
            op.setdefault("ndma", 1)
        pr.schedule()
        pr.emit(nc, stack)
    return nc


_CACHE = {}


def _t5_bucket(n):
    max_exact = 16
    large = max_exact + (np.log(np.maximum(n, 1) / max_exact) / np.log(128 / max_exact) * (32 - max_exact)).astype(np.int32)
    large = np.minimum(large, 31)
    return np.where(n < max_exact, n, large).astype(np.int32)


def kernel(x, p, w_in, b_forget, w_out, rel_bias, swa_sinks, g_attn_pre, g_attn_post,
           w_ff1, w_ff2, g_ff_pre, g_ff_post, w_ple, w_ple_gate, g_ple_post):
    f = lambda a: np.ascontiguousarray(np.asarray(a, dtype=np.float32))
    x = f(x); p = f(p)
    B = x.shape[0]
    if "nc" not in _CACHE:
        _CACHE["nc"] = build_program()
    nc = _CACHE["nc"]
    gs = np.stack([f(g_attn_pre)[0], f(g_attn_post)[0], f(g_ff_pre)[0], f(g_ff_post)[0], f(g_ple_post)[0]])
    gT = np.ascontiguousarray(gs.reshape(5, 8, 128).transpose(2, 0, 1).reshape(128, 40))
    rb = f(rel_bias)
    jj = np.arange(128)[:, None]
    ii = np.arange(128)[None, :]
    dist_prev = ii + 128 - jj
    dist_cur = ii - jj
    bk_prev = _t5_bucket(np.clip(dist_prev, 0, None))
    bk_cur = _t5_bucket(np.clip(dist_cur, 0, None))
    valid_prev = (dist_prev >= 0) & (dist_prev < 128)
    valid_cur = (dist_cur >= 0) & (dist_cur < 128)
    rb_ext = np.concatenate([rb, np.full((1, rb.shape[1]), NEG, np.float32)], axis=0)
    ix_prev = np.where(valid_prev, bk_prev, rb.shape[0])
    ix_cur = np.where(valid_cur, bk_cur, rb.shape[0])
    biasT = np.zeros((128, 2, 2, 4, 128), np.float32)
    for g in range(2):
        heads = [4 * g, 4 * g + 2, 4 * g + 1, 4 * g + 3]
        for sl, h in enumerate(heads):
            biasT[:, 0, g, sl, :] = rb_ext[ix_prev, h]
            biasT[:, 1, g, sl, :] = rb_ext[ix_cur, h]
    biasT = biasT.reshape(128, 2048)
    maskF = np.where(jj > ii, NEG, 0.0).astype(np.float32)
    ident = np.eye(128, dtype=np.float32)
    common = dict(w_in=f(w_in)[0], w_out=f(w_out)[0], w_ff1=f(w_ff1)[0], w_ff2=f(w_ff2)[0],
                  w_ple=f(w_ple)[0], w_gate=f(w_ple_gate)[0], gT=gT, bfg=f(b_forget).reshape(8, 1),
                  sinks=np.ascontiguousarray(np.tile(f(swa_sinks).reshape(1, 8), (128, 1))), biasT=biasT, maskF=maskF, ident=ident)
    in_maps = []
    for b in range(B):
        m = dict(common)
        m["xT"] = np.ascontiguousarray(x[b].T)
        m["pT"] = np.ascontiguousarray(p[0, b].T)
        in_maps.append(m)
    res = run_bass_kernel_spmd(nc, in_maps, core_ids=list(range(B)))
    out = np.stack([np.ascontiguousarray(res.results[b]["yT"].T) for b in range(B)]).astype(np.float32)
    return out
```
